# Optimizing a Trainium2 kernel written in Bass

```python
import math
import jax, jax.numpy as jnp
from jax import lax
import numpy as np

D_MODEL = 1024
BATCH = 2
SEQ = 16384
DEPTH = 2
DEC_BATCH = 8
DEC_SEQ = 8192
PAST_LEN = 128

HEAD_DIM = 64
POOL_WINDOWS = (2, 4, 8, 16)
POOL_GROUPS = len(POOL_WINDOWS)
POOL_WIDTH = D_MODEL // 4
POOL_GROUP_DIM = POOL_WIDTH // POOL_GROUPS
ATTN_WIDTH = D_MODEL // 2
ATTN_HEADS = ATTN_WIDTH // HEAD_DIM
DILATED_PATTERNS = ((128, 1), (512, 4), (2048, 16))
ROT_DIM = HEAD_DIM // 4
ROPE_THETA = 500000.0
SGU_WIDTH = D_MODEL // 4
SGU_GROUPS = 4
SGU_GROUP_DIM = SGU_WIDTH // SGU_GROUPS
SGU_CHUNK = 128
MIX_WIDTH = POOL_WIDTH + ATTN_WIDTH + SGU_WIDTH
PROJ_WIDTH = POOL_WIDTH + 3 * ATTN_WIDTH + 2 * SGU_WIDTH
D_FF = 4 * D_MODEL
N_MOD = 6
EPS = 1e-6
MASK_VALUE = -1e30

kernel_name = "hybrid_pool_dilated_sgu_encoder"


def rms_norm(x, g):
    xf = x.astype(jnp.float32)
    y = xf * lax.rsqrt(jnp.mean(xf * xf, axis=-1, keepdims=True) + EPS)
    return (y * g.astype(jnp.float32)).astype(x.dtype)


def pool_mixer(h, pool_w, pool_scale):
    B, S, _ = h.shape
    hf = h.astype(jnp.float32).reshape(B, S, POOL_GROUPS, POOL_GROUP_DIM)
    cs = jnp.concatenate([jnp.zeros((B, 1, POOL_GROUPS, POOL_GROUP_DIM), jnp.float32),
                          jnp.cumsum(hf, axis=1)], axis=1)
    pos = jnp.arange(S)
    outs = []
    for g, win in enumerate(POOL_WINDOWS):
        lo = jnp.clip(pos - win // 2, 0, S)
        hi = jnp.clip(pos + win // 2, 0, S)
        win_sum = cs[:, hi, g] - cs[:, lo, g]
        cnt = (hi - lo).astype(jnp.float32)[None, :, None]
        outs.append(win_sum / cnt - hf[:, :, g])
    p = jnp.stack(outs, axis=2).astype(h.dtype)
    y = jnp.einsum('bsgc,gcd->bsgd', p, pool_w).reshape(B, S, POOL_WIDTH)
    return y * pool_scale


def partial_rotary(x, pos):
    inv_freq = ROPE_THETA ** (-jnp.arange(0, ROT_DIM, 2, dtype=jnp.float32) / ROT_DIM)
    ang = pos.astype(jnp.float32)[:, None] * inv_freq[None, :]
    cos = jnp.cos(ang)[None, :, None, :]
    sin = jnp.sin(ang)[None, :, None, :]
    xf = x.astype(jnp.float32)
    x1 = xf[..., :ROT_DIM // 2]
    x2 = xf[..., ROT_DIM // 2:ROT_DIM]
    out = jnp.concatenate([x1 * cos - x2 * sin, x2 * cos + x1 * sin, xf[..., ROT_DIM:]], axis=-1)
    return out.astype(x.dtype)


def band_attention(q, k, v, half):
    N, L, H, dh = q.shape
    blk = half
    nb = -(-L // blk)
    Lp = nb * blk
    qb = jnp.pad(q, ((0, 0), (0, Lp - L), (0, 0), (0, 0))).reshape(N, nb, blk, H, dh)

    def neighbourhood(t):
        tb = jnp.pad(t, ((0, 0), (blk, Lp - L + blk), (0, 0), (0, 0))).reshape(N, nb + 2, blk, H, dh)
        return jnp.concatenate([tb[:, :-2], tb[:, 1:-1], tb[:, 2:]], axis=2)

    kb = neighbourhood(k)
    vb = neighbourhood(v)
    qpos = jnp.arange(Lp).reshape(nb, blk)
    kpos = jnp.arange(nb)[:, None] * blk - blk + jnp.arange(3 * blk)[None, :]
    valid = ((jnp.abs(qpos[:, :, None] - kpos[:, None, :]) <= half)
             & (kpos[:, None, :] >= 0) & (kpos[:, None, :] < L))
    s = jnp.einsum('nbqhd,nbkhd->nbhqk', qb, kb).astype(jnp.float32) * (dh ** -0.5)
    s = jnp.where(valid[None, :, None], s, MASK_VALUE)
    m = jnp.max(s, axis=-1)
    p = jnp.exp(s - m[..., None])
    l = jnp.sum(p, axis=-1)
    o = jnp.einsum('nbhqk,nbkhd->nbqhd', p, vb.astype(jnp.float32))
    o = o / jnp.transpose(l, (0, 1, 3, 2))[..., None]
    o = o.reshape(N, Lp, H, dh)[:, :L]
    m = jnp.transpose(m, (0, 1, 3, 2)).reshape(N, Lp, H)[:, :L]
    l = jnp.transpose(l, (0, 1, 3, 2)).reshape(N, Lp, H)[:, :L]
    return o, m, l


def dilated_attention(q, k, v):
    B, S, H, dh = q.shape
    outs, ms, ls = [], [], []
    for win, d in DILATED_PATTERNS:
        half = win // (2 * d)
        Ld = S // d

        def fold(t):
            return t.reshape(B, Ld, d, H, dh).transpose(0, 2, 1, 3, 4).reshape(B * d, Ld, H, dh)

        o, m, l = band_attention(fold(q), fold(k), fold(v), half)
        outs.append(o.reshape(B, d, Ld, H, dh).transpose(0, 2, 1, 3, 4).reshape(B, S, H, dh))
        ms.append(m.reshape(B, d, Ld, H).transpose(0, 2, 1, 3).reshape(B, S, H))
        ls.append(l.reshape(B, d, Ld, H).transpose(0, 2, 1, 3).reshape(B, S, H))
    m_all = jnp.stack(ms)
    wts = jnp.stack(ls) * jnp.exp(m_all - jnp.max(m_all, axis=0, keepdims=True))
    o = jnp.sum(wts[..., None] * jnp.stack(outs), axis=0) / jnp.sum(wts, axis=0)[..., None]
    return o


def spatial_gating(z, sgu_w, sgu_b):
    B, S, _ = z.shape
    u, v = z[..., :SGU_WIDTH], z[..., SGU_WIDTH:]
    vf = v.astype(jnp.float32).reshape(B, S // SGU_CHUNK, SGU_CHUNK, SGU_GROUPS, SGU_GROUP_DIM)
    mu = jnp.mean(vf, axis=-1, keepdims=True)
    var = jnp.mean(jnp.square(vf - mu), axis=-1, keepdims=True)
    vn = ((vf - mu) * lax.rsqrt(var + EPS)).astype(z.dtype)
    vm = jnp.einsum('gpq,bnqgc->bnpgc', sgu_w, vn) + jnp.transpose(sgu_b)[None, None, :, :, None]
    return u * vm.reshape(B, S, SGU_WIDTH)


def mixer(h, w_in, pool_w, pool_scale, sgu_w, sgu_b, w_out):
    B, S, _ = h.shape
    z = h @ w_in
    o1 = POOL_WIDTH
    o2 = o1 + ATTN_WIDTH
    o3 = o2 + ATTN_WIDTH
    o4 = o3 + ATTN_WIDTH
    za, zq, zk, zv, zc = z[..., :o1], z[..., o1:o2], z[..., o2:o3], z[..., o3:o4], z[..., o4:]
    ya = pool_mixer(za, pool_w, pool_scale)
    pos = jnp.arange(S)
    q = partial_rotary(zq.reshape(B, S, ATTN_HEADS, HEAD_DIM), pos)
    k = partial_rotary(zk.reshape(B, S, ATTN_HEADS, HEAD_DIM), pos)
    v = zv.reshape(B, S, ATTN_HEADS, HEAD_DIM)
    yb = dilated_attention(q, k, v).astype(h.dtype).reshape(B, S, ATTN_WIDTH)
    yc = spatial_gating(jax.nn.gelu(zc), sgu_w, sgu_b)
    return jnp.concatenate([ya, yb, yc], axis=-1) @ w_out


def trunk(x, c, w_ada, b_ada, g_mix, g_mlp, w_in, pool_w, pool_scale, sgu_w, sgu_b,
          w_out, w_up, w_down, g_final):
    c_act = jax.nn.silu(c)
    for l in range(DEPTH):
        mod = (c_act @ w_ada[l] + b_ada[l])[:, None, :]
        sh1, sc1, gt1, sh2, sc2, gt2 = jnp.split(mod, N_MOD, axis=-1)
        h = rms_norm(x, g_mix[l]) * (1 + sc1) + sh1
        x = x + gt1 * mixer(h, w_in[l], pool_w[l], pool_scale[l], sgu_w[l], sgu_b[l], w_out[l])
        h = rms_norm(x, g_mlp[l]) * (1 + sc2) + sh2
        x = x + gt2 * (jnp.square(jax.nn.relu(h @ w_up[l])) @ w_down[l])
    return rms_norm(x, g_final)


def setup_inputs(seed: int = 0) -> dict:
    key = jax.random.key(seed)
    ks = jax.random.split(key, 20)
    f32 = jnp.float32
    nrm = lambda k, shape, s: jax.random.normal(k, shape, f32) * s
    return {
        "x_prompt": nrm(ks[0], (BATCH, SEQ, D_MODEL), 1.0),
        "x_sample": nrm(ks[1], (DEC_BATCH, DEC_SEQ, D_MODEL), 1.0),
        "c_prompt": nrm(ks[2], (BATCH, D_MODEL), 1.0),
        "c_sample": nrm(ks[3], (DEC_BATCH, D_MODEL), 1.0),
        "w_ada": nrm(ks[4], (DEPTH, D_MODEL, N_MOD * D_MODEL), 0.5 * D_MODEL ** -0.5),
        "b_ada": nrm(ks[5], (DEPTH, N_MOD * D_MODEL), 0.02),
        "g_mix": 1.0 + nrm(ks[6], (DEPTH, D_MODEL), 0.02),
        "g_mlp": 1.0 + nrm(ks[7], (DEPTH, D_MODEL), 0.02),
        "w_in": nrm(ks[8], (DEPTH, D_MODEL, PROJ_WIDTH), D_MODEL ** -0.5),
        "pool_w": nrm(ks[9], (DEPTH, POOL_GROUPS, POOL_GROUP_DIM, POOL_GROUP_DIM), POOL_GROUP_DIM ** -0.5),
        "pool_scale": 1.0 + nrm(ks[10], (DEPTH, POOL_WIDTH), 0.02),
        "sgu_w": nrm(ks[11], (DEPTH, SGU_GROUPS, SGU_CHUNK, SGU_CHUNK), SGU_CHUNK ** -0.5),
        "sgu_b": 1.0 + nrm(ks[12], (DEPTH, SGU_GROUPS, SGU_CHUNK), 0.02),
        "w_out": nrm(ks[13], (DEPTH, MIX_WIDTH, D_MODEL), MIX_WIDTH ** -0.5),
        "w_up": nrm(ks[14], (DEPTH, D_MODEL, D_FF), D_MODEL ** -0.5),
        "w_down": nrm(ks[15], (DEPTH, D_FF, D_MODEL), D_FF ** -0.5),
        "g_final": 1.0 + nrm(ks[16], (D_MODEL,), 0.02),
    }


def reference(x_prompt, x_sample, c_prompt, c_sample, w_ada, b_ada, g_mix, g_mlp, w_in, pool_w,
              pool_scale, sgu_w, sgu_b, w_out, w_up, w_down, g_final):
    y_prompt = trunk(x_prompt, c_prompt, w_ada, b_ada, g_mix, g_mlp, w_in, pool_w, pool_scale,
                     sgu_w, sgu_b, w_out, w_up, w_down, g_final)
    y_sample = trunk(x_sample, c_sample, w_ada, b_ada, g_mix, g_mlp, w_in, pool_w, pool_scale,
                     sgu_w, sgu_b, w_out, w_up, w_down, g_final)
    return (y_prompt, y_sample)
```

```python
import numpy as np
import concourse.bass as bass
import concourse.mybir as mybir
from concourse.bass_utils import run_bass_kernel_spmd

F32 = mybir.dt.float32
BF16 = mybir.dt.bfloat16
ALU = mybir.AluOpType
AF = mybir.ActivationFunctionType

D = 1024
DFF = 4096
DEPTH = 2
EPS = 1e-6
NCORES = 8
WIN_START = (0, 2048, 6144, 8192)
WIN_OWN = (0, 2048, 2048, 4096)


class Buf:
    __slots__ = ("name", "writers", "rc", "rd", "carry")

    def __init__(self, name):
        self.name = name
        self.writers = []
        self.rc = {}
        self.rd = []
        self.carry = []


class Op:
    __slots__ = ("eng", "fn", "deps", "sig", "is_dma", "sem", "val", "k", "cells")

    def __init__(self, eng, fn, is_dma):
        self.eng = eng
        self.fn = fn
        self.deps = []
        self.sig = False
        self.is_dma = is_dma
        self.sem = None
        self.val = 0


class Graph:
    def __init__(self, nc, n_dma_sems=24):
        self.nc = nc
        self.ops = []
        self.streams = {"pe": [], "act": [], "dve": [], "pool": [], "sp": []}
        self.n_dma_sems = n_dma_sems
        self.rr = 0
        self.dlast = [None] * n_dma_sems
        self.pending = {e: [] for e in self.streams}

    def barrier(self):
        deps = []
        for e in ("pe", "act", "dve", "pool"):
            comp = [o for o in self.streams[e] if not o.is_dma]
            if comp:
                comp[-1].sig = True
                deps.append(comp[-1])
        deps += [d for d in self.dlast if d is not None]
        for e in self.streams:
            self.pending[e] = list(deps)

    def op(self, eng, fn, reads=(), writes=(), dma=False, multi=False):
        o = Op(eng, fn, dma)
        cl = fn.__closure__ or ()
        o.cells = [(n, c.cell_contents) for n, c in zip(fn.__code__.co_freevars, cl)]
        deps = {}
        for b in reads:
            for w in b.writers:
                deps[id(w)] = (w, True)
        for b in writes:
            prior = list(b.rc.values()) + b.rd
            if multi:
                prior += b.carry
            else:
                prior += b.writers
                b.carry = list(b.rc.values()) + b.rd + b.writers
            for r in prior:
                if id(r) not in deps:
                    deps[id(r)] = (r, False)
        for d, raw in deps.values():
            if not d.is_dma and not dma and d.eng == eng:
                if eng == "pe" or not raw:
                    continue
            o.deps.append(d)
            d.sig = True
        if self.pending[eng]:
            for d in self.pending[eng]:
                if d.is_dma or d.eng != eng:
                    o.deps.append(d)
            self.pending[eng] = []
        if dma:
            k = self.rr % self.n_dma_sems
            self.rr += 1
            if self.dlast[k] is not None:
                o.deps.append(self.dlast[k])
            o.k = k
            self.dlast[k] = o
        for b in writes:
            if multi:
                b.writers.append(o)
            else:
                b.writers = [o]
            b.rc = {}
            b.rd = []
        for b in reads:
            if o not in b.writers:
                if dma:
                    b.rd.append(o)
                else:
                    b.rc[eng] = o
        self.ops.append(o)
        self.streams[eng].append(o)
        return o

    def emit(self, final_wait_ops=()):
        nc = self.nc
        csem = {e: nc.alloc_semaphore(f"s_{e}") for e in self.streams}
        dsems = [nc.alloc_semaphore(f"s_dma{i}") for i in range(self.n_dma_sems)]
        cnt = {e: 0 for e in self.streams}
        dcnt = [0] * self.n_dma_sems
        for o in self.ops:
            if o.is_dma:
                dcnt[o.k] += 16
                o.sem, o.val = dsems[o.k], dcnt[o.k]
            elif o.sig:
                cnt[o.eng] += 1
                o.sem, o.val = csem[o.eng], cnt[o.eng]
        stats = {e: len(s) for e, s in self.streams.items()}
        stats["waits"] = 0

        def run_stream(ename, eng, extra_final=()):
            waited = {}
            for o in self.streams[ename]:
                for d in o.deps:
                    if waited.get(d.sem.num, 0) >= d.val:
                        continue
                    eng.wait_ge(d.sem, d.val)
                    waited[d.sem.num] = d.val
                    stats["waits"] += 1
                for (n, v0), c in zip(o.cells, o.fn.__closure__ or ()):
                    if c.cell_contents is not v0:
                        raise RuntimeError(f"late-bound closure variable {n!r} changed after recording "
                                           f"({o.fn.__code__.co_filename}:{o.fn.__code__.co_firstlineno}); bind it as a lambda default")
                ins = o.fn(eng)
                if o.sem is not None:
                    ins.then_inc(o.sem, 16 if o.is_dma else 1)
            for d in extra_final:
                if waited.get(d.sem.num, 0) < d.val:
                    eng.wait_ge(d.sem, d.val)
                    waited[d.sem.num] = d.val

        with nc.Block() as block:
            @block.tensor
            def _(e):
                run_stream("pe", e)

            @block.scalar
            def _(e):
                run_stream("act", e)

            @block.vector
            def _(e):
                run_stream("dve", e)

            @block.gpsimd
            def _(e):
                run_stream("pool", e)

            @block.sync
            def _(e):
                run_stream("sp", e, extra_final=final_wait_ops)
        return stats


POOL_W = 256
QO, KO, VO, ZCU, ZCV = 256, 768, 1280, 1792, 2048
PROJ = 2304


def build(seg_len=8192, nseg=2, stage="full", debug=False):
    S = seg_len
    nc = bass.Bass("TRN2", target_bir_lowering=False)
    g = Graph(nc)

    def din(name, shape, dt=F32):
        return nc.dram_tensor(name, list(shape), dt, kind="ExternalInput")

    x_in = din("x", [nseg, S, D])
    c_in = din("c", [nseg, 128, 8])
    w_ada = din("w_ada", [DEPTH, D, 6 * D])
    b_ada = din("b_ada", [DEPTH, 6 * D])
    g_mix = din("g_mix", [DEPTH, D])
    g_mlp = din("g_mlp", [DEPTH, D])
    w_in = din("w_in", [DEPTH, D, PROJ])
    w_up = din("w_up", [DEPTH, D, DFF])
    w_down = din("w_down", [DEPTH, DFF, D])
    g_final = din("g_final", [1, D])
    pool_w = din("pool_w", [DEPTH, 4, 64, 64])
    pool_sc = din("pool_scale", [DEPTH, 128, 2])
    sgu_w = din("sgu_w", [DEPTH, 4, 128, 128])
    sgu_b = din("sgu_b", [DEPTH, 4, 128])
    w_out = din("w_out", [DEPTH, D, D])
    pool_ic = din("pool_ic", [128, 2, 2, 8])
    pool_iw = din("pool_iw", [128, 2])
    ident_in = din("ident", [128, 128], BF16)
    rope_in = din("rope", [S, 16])
    y_out = nc.dram_tensor("y", [nseg, S, D], F32, kind="ExternalOutput")
    xres = nc.dram_tensor("xres", [nseg, S, D], F32, kind="Internal")
    x1dbg = nc.dram_tensor("x1dbg", [nseg, S, D], F32, kind="ExternalOutput" if (debug and stage == "full") else "Internal")
    modbc = nc.dram_tensor("modbc", [nseg, DEPTH, 128, 6 * D], F32, kind="Internal")
    dbg_kind = ("ExternalInput" if stage == "attn" else ("ExternalOutput" if (stage.startswith("inproj") or stage == "full") else "Internal")) if debug else "Internal"
    masks_in = din("masks", [128, 4, 256], BF16)
    ones_in = din("ones64", [128, 64], BF16)
    dyb = nc.dram_tensor("dyb", [nseg, 8, 64, S], BF16, kind="ExternalOutput" if (debug and stage in ("attn", "full")) else "Internal")
    qd = nc.dram_tensor("dq", [nseg, S, 512], BF16, kind=dbg_kind)
    kd = nc.dram_tensor("dk", [nseg, S, 512], BF16, kind=dbg_kind)
    vd = nc.dram_tensor("dv", [nseg, S, 512], BF16, kind=dbg_kind)
    mixA = nc.dram_tensor("mixA", [nseg, 256, S], BF16, kind="ExternalOutput" if (debug and stage == "full") else "Internal")
    mixC = nc.dram_tensor("mixC", [nseg, 256, S], BF16, kind="ExternalOutput" if (debug and stage == "full") else "Internal")
    zaD = nc.dram_tensor("zaD", [nseg, 128, 2, S + 16], F32, kind="Internal")
    b_zaD = [Buf(f"zaD{s}") for s in range(nseg)]
    b_mixA = [Buf(f"mixA{s}") for s in range(nseg)]
    b_mixC = [Buf(f"mixC{s}") for s in range(nseg)]
    b_qd = [Buf(f"qd{s}") for s in range(nseg)]
    b_kd = [Buf(f"kd{s}") for s in range(nseg)]
    b_vd = [Buf(f"vd{s}") for s in range(nseg)]

    def sb(name, shape, dt):
        return nc.alloc_sbuf_tensor("sb_" + name, shape, dt)

    ident = sb("ident", [128, 128], BF16)
    b_ident = Buf("ident")
    g.op("sp", lambda e: e.dma_start(out=ident[:], in_=ident_in.ap()), writes=[b_ident], dma=True)
    Gm = sb("Gm", [128, D], F32)
    SHm = sb("SHm", [128, D], F32)
    GTm = sb("GTm", [128, D], F32)
    gf_bc = sb("gf_bc", [128, D], F32)
    b_G, b_SH, b_GT, b_gf = Buf("G"), Buf("SH"), Buf("GT"), Buf("gf")
    g.op("sp", lambda e: e.dma_start(out=gf_bc[:], in_=g_final[0:1, :].partition_broadcast(128)), writes=[b_gf], dma=True)
    NXB = 3
    xt = [sb(f"xt{i}", [128, D], F32) for i in range(NXB)]
    b_xt = [Buf(f"xt{i}") for i in range(NXB)]
    tmpf = [sb(f"tmpf{i}", [128, D], F32) for i in range(2)]
    b_tmpf = [Buf("tmpf0"), Buf("tmpf1")]
    hb = [sb(f"hb{i}", [128, D], BF16) for i in range(2)]
    b_hb = [Buf("hb0"), Buf("hb1")]
    yo1 = sb("yo", [128, D], F32)
    b_yo1 = Buf("yo")
    junk = sb("junk", [128, D], BF16)
    b_junk = Buf("junk")
    ss = [sb(f"ss{i}", [128, 4], F32) for i in range(2)]
    rs = [sb(f"rs{i}", [128, 4], F32) for i in range(2)]
    b_ss = [Buf("ss0"), Buf("ss1")]
    b_rs = [Buf("rs0"), Buf("rs1")]
    fs = [sb(f"fs{i}", [128, 2], F32) for i in range(2)]
    b_fs = [Buf("fs0"), Buf("fs1")]

    ARENA = 159 * 1024
    arena = sb("arena", [128, ARENA // 2], BF16)
    arena_f = arena.bitcast(F32)

    hiwater = {}

    class Carver:
        def __init__(self, tag="?"):
            self.off = 0
            self.tag = tag

        def take(self, shape, dt):
            esz = 2 if dt == BF16 else 4
            n = 1
            for d_ in shape[1:]:
                n *= d_
            self.off = (self.off + 31) // 32 * 32
            assert self.off + n * esz <= ARENA, (self.off, n * esz, ARENA)
            base = arena if dt == BF16 else arena_f
            e0 = self.off // esz
            v = base[:, e0:e0 + n]
            self.off += n * esz
            hiwater[self.tag] = max(hiwater.get(self.tag, 0), self.off)
            if len(shape) == 3:
                v = v.rearrange("p (a b) -> p a b", a=shape[1])
            elif len(shape) == 4:
                v = v.rearrange("p (a b c) -> p a b c", a=shape[1], b=shape[2])
            return v

    banks = [nc.alloc_psum_tensor(f"ps_bank{i}", [128, 512], F32) for i in range(8)]
    banksT = [b.bitcast(BF16) for b in banks]
    b_banks = [Buf(f"bank{i}") for i in range(8)]
    bank_rr = [0]

    def next_bank():
        i = bank_rr[0] % 8
        bank_rr[0] += 1
        return banks[i], banksT[i], b_banks[i]

    cv = Carver()
    c_sb = cv.take([128, nseg, 8], F32)
    c_act = cv.take([128, nseg, 8], F32)
    c_rep = cv.take([128, nseg, 8, 128], BF16)
    wa = [cv.take([128, 8, 512], BF16) for _ in range(2)]
    bb_sb = [cv.take([128, 512], F32) for _ in range(2)]
    mo = [cv.take([128, 512], F32) for _ in range(2)]
    b_c, b_cact, b_crep = Buf("c"), Buf("cact"), Buf("crep")
    b_wa = [Buf("wa0"), Buf("wa1")]
    b_bb = [Buf("bb0"), Buf("bb1")]
    b_mo = [Buf("mo0"), Buf("mo1")]
    b_modbc = Buf("modbc")
    g.op("sp", lambda e: e.dma_start(out=c_sb, in_=c_in.ap().rearrange("s p k -> p s k")), writes=[b_c], dma=True)
    g.op("act", lambda e: e.activation(out=c_act, in_=c_sb, func=AF.Silu), reads=[b_c], writes=[b_cact])
    g.op("dve", lambda e: e.tensor_copy(out=c_rep, in_=c_act.unsqueeze(3).to_broadcast([128, nseg, 8, 128])),
         reads=[b_cact], writes=[b_crep])
    it = 0
    for l in range(DEPTH):
        for cb in range(12):
            k = it % 2
            it += 1
            g.op("pool", lambda e, l=l, cb=cb, k=k: e.dma_start(
                out=wa[k], in_=w_ada[l, :, cb * 512:(cb + 1) * 512].rearrange("(kc p) n -> p kc n", p=128)),
                writes=[b_wa[k]], dma=True)
            g.op("sp", lambda e, l=l, cb=cb, k=k: e.dma_start(
                out=bb_sb[k], in_=b_ada[l:l + 1, cb * 512:(cb + 1) * 512].partition_broadcast(128)),
                writes=[b_bb[k]], dma=True)
            for s in range(nseg):
                bk, _, b_bk = next_bank()
                for kc in range(8):
                    g.op("pe", lambda e, bk=bk, s=s, kc=kc, k=k: e.matmul(
                        bk[:], lhsT=c_rep[:, s, kc, :], rhs=wa[k][:, kc, :], start=(kc == 0), stop=(kc == 7)),
                        reads=[b_crep, b_wa[k]], writes=[b_bk])
                m = (it + s) % 2
                g.op("dve", lambda e, bk=bk, k=k, m=m: e.tensor_tensor(out=mo[m], in0=bk[:], in1=bb_sb[k], op=ALU.add),
                     reads=[b_bk, b_bb[k]], writes=[b_mo[m]])
                g.op("sp", lambda e, s=s, l=l, cb=cb, m=m: e.dma_start(
                    out=modbc[s, l, :, cb * 512:(cb + 1) * 512], in_=mo[m]), reads=[b_mo[m]], writes=[b_modbc], dma=True, multi=True)

    b_xres = [Buf(f"xres{s}") for s in range(nseg)]
    finals = []
    cnt = {"tt": 0}

    def load_mod(s, l, sub, gvec):
        o = 3 * sub * D
        g.op("sp", lambda e: e.dma_start(out=tmpf[1][:], in_=gvec[l:l + 1, :].partition_broadcast(128)), writes=[b_tmpf[1]], dma=True)
        g.op("sp", lambda e: e.dma_start(out=SHm[:], in_=modbc[s, l, :, o:o + D]), reads=[b_modbc], writes=[b_SH], dma=True)
        g.op("sp", lambda e: e.dma_start(out=Gm[:], in_=modbc[s, l, :, o + D:o + 2 * D]), reads=[b_modbc], writes=[b_G], dma=True)
        g.op("sp", lambda e: e.dma_start(out=GTm[:], in_=modbc[s, l, :, o + 2 * D:o + 3 * D]), reads=[b_modbc], writes=[b_GT], dma=True)
        g.op("dve", lambda e: e.scalar_tensor_tensor(out=Gm[:], in0=Gm[:], scalar=1.0, in1=tmpf[1][:], op0=ALU.add, op1=ALU.mult),
             reads=[b_G, b_tmpf[1]], writes=[b_G])

    def norm_a(src_ap_fn, s, tg, tgsize, src_bufs, hbs, b_hbs):
        nt = tgsize // 128
        assert len(hbs) >= nt
        k2 = tg % 2
        g.op("dve", lambda e: e.memset(ss[k2][:], 0.0), writes=[b_ss[k2]])
        for j in range(nt):
            xi = cnt["tt"] % NXB
            cnt["tt"] += 1
            t0 = tg * tgsize + j * 128
            g.op("sp", lambda e, xi=xi, t0=t0: e.dma_start(out=xt[xi][:], in_=src_ap_fn(s, t0)),
                 reads=src_bufs, writes=[b_xt[xi]], dma=True)
            g.op("act", lambda e, xi=xi, j=j: e.activation(out=junk[:], in_=xt[xi][:], func=AF.Square,
                                                          accum_out=ss[k2][:, j:j + 1]),
                 reads=[b_xt[xi]], writes=[b_junk, b_ss[k2]])
            g.op("dve", lambda e, j=j: e.tensor_scalar(out=rs[k2][:, j:j + 1], in0=ss[k2][:, j:j + 1], scalar1=1.0 / D, scalar2=EPS,
                                                       op0=ALU.mult, op1=ALU.add), reads=[b_ss[k2]], writes=[b_rs[k2]])
            g.op("act", lambda e, j=j: e.activation(out=rs[k2][:, j:j + 1], in_=rs[k2][:, j:j + 1], func=AF.Sqrt),
                 reads=[b_rs[k2]], writes=[b_rs[k2]])
            g.op("dve", lambda e, j=j: e.reciprocal(out=rs[k2][:, j:j + 1], in_=rs[k2][:, j:j + 1]), reads=[b_rs[k2]], writes=[b_rs[k2]])
            m = j % 2
            g.op("dve", lambda e, xi=xi, j=j, m=m: e.scalar_tensor_tensor(
                out=tmpf[m][:], in0=xt[xi][:], scalar=rs[k2][:, j:j + 1], in1=Gm[:], op0=ALU.mult, op1=ALU.mult),
                reads=[b_xt[xi], b_rs[k2], b_G], writes=[b_tmpf[m]])
            g.op("pool", lambda e, m=m, j=j: e.tensor_tensor(out=hbs[j], in0=tmpf[m][:], in1=SHm[:], op=ALU.add),
                 reads=[b_tmpf[m], b_SH], writes=[b_hbs[j]])

    def norm_b(tgsize, hTk, b_hTk, hbs, b_hbs):
        for j in range(tgsize // 128):
            bk, bkT, b_bk = next_bank()
            for kc in range(8):
                g.op("pe", lambda e, bkT=bkT, j=j, kc=kc: e.transpose(
                    out=bkT[:, kc * 128:(kc + 1) * 128], in_=hbs[j][:, kc * 128:(kc + 1) * 128], identity=ident[:]),
                    reads=[b_hbs[j], b_ident], writes=[b_bk])
            g.op("act", lambda e, bkT=bkT, j=j: e.copy(
                out=hTk[:, :, j * 128:(j + 1) * 128], in_=bkT[:, 0:1024].rearrange("p (k t) -> p k t", k=8)),
                reads=[b_bk], writes=[b_hTk], multi=(j != 0))

    def phase_inproj(l, s, src_is_input, rope_mode="full"):
        use_rope = rope_mode == "full"
        g.barrier()
        cv = Carver("P1 inproj")
        win = cv.take([128, 8, PROJ], BF16)
        hT = [cv.take([128, 8, 512], BF16) for _ in range(2)]
        hbs = [hb[0][:], hb[1][:], cv.take([128, D], BF16), cv.take([128, D], BF16)]
        b_hbs = [b_hb[0], b_hb[1], Buf("hb2"), Buf("hb3")]
        qb = [cv.take([128, 8, 64], BF16) for _ in range(2)]
        kb = [cv.take([128, 8, 64], BF16) for _ in range(2)]
        vb = [cv.take([128, 512], BF16) for _ in range(2)]
        rp = [cv.take([128, 16], F32) for _ in range(2)]
        rt = [cv.take([128, 8, 8], F32) for _ in range(4)]
        b_win = Buf("win")
        b_hT = [Buf("hT0"), Buf("hT1")]
        b_qb = [Buf("qb0"), Buf("qb1")]
        b_kb = [Buf("kb0"), Buf("kb1")]
        b_vb = [Buf("vb0"), Buf("vb1")]
        b_rp = [Buf("rp0"), Buf("rp1")]
        b_rt = [Buf(f"rt{i}") for i in range(4)]
        g.op("pool", lambda e: e.dma_start(out=win, in_=w_in[l].rearrange("(kc p) n -> p kc n", p=128)), writes=[b_win], dma=True)
        load_mod(s, l, 0, g_mix)
        full = rope_mode == "full"
        if full:
            zst = [cv.take([128, 2, 512], F32) for _ in range(2)]
            zch = [cv.take([128, 2, 528], F32) for _ in range(2)]
            zz8 = cv.take([128, 2, 8], F32)
            T1 = cv.take([128, 2, 528], F32)
            T2 = cv.take([128, 2, 528], F32)
            T3 = cv.take([128, 2, 528], F32)
            Sb = cv.take([128, 2, 512], F32)
            pbf = cv.take([128, 2, 512], BF16)
            yab = [cv.take([128, 2, 512], BF16) for _ in range(2)]
            pwbd = cv.take([128, 2, 128], BF16)
            psc = cv.take([128, 2], F32)
            pic = cv.take([128, 2, 2, 8], F32)
            piw = cv.take([128, 2], F32)
            etmp = cv.take([128, 2, 8], F32)
            b_T1, b_T2, b_T3, b_Sb, b_pbf = Buf("T1"), Buf("T2"), Buf("T3"), Buf("Sb"), Buf("pbf")
            b_zst = [Buf("zst0"), Buf("zst1")]
            b_zch = [Buf("zch0"), Buf("zch1")]
            b_zz8 = Buf("zz8")
            b_yab = [Buf("yab0"), Buf("yab1")]
            b_pw, b_psc, b_pic, b_piw, b_etmp = Buf("pwbd"), Buf("psc"), Buf("pic"), Buf("piw"), Buf("etmp")
            g.op("pool", lambda e: e.memset(pwbd, 0.0), writes=[b_pw])
            for gi in range(4):
                g.op("pool", lambda e, gi=gi: e.dma_start(out=pwbd[(gi % 2) * 64:(gi % 2) * 64 + 64, gi // 2, (gi % 2) * 64:(gi % 2) * 64 + 64],
                                                        in_=pool_w[l, gi]), writes=[b_pw], dma=True, multi=True)
            g.op("sp", lambda e: e.dma_start(out=psc, in_=pool_sc[l]), writes=[b_psc], dma=True)
            g.op("sp", lambda e: e.dma_start(out=pic, in_=pool_ic.ap()), writes=[b_pic], dma=True)
            g.op("sp", lambda e: e.dma_start(out=piw, in_=pool_iw.ap()), writes=[b_piw], dma=True)
            g.op("pool", lambda e: e.memset(zz8, 0.0), writes=[b_zz8])
            g.op("sp", lambda e: e.dma_start(out=zaD[s, :, :, 0:8], in_=zz8), reads=[b_zz8], writes=[b_zaD[s]], dma=True)
            g.op("sp", lambda e: e.dma_start(out=zaD[s, :, :, S + 8:S + 16], in_=zz8), reads=[b_zz8], writes=[b_zaD[s]], dma=True, multi=True)
            swn = cv.take([128, 4, 128], BF16)
            swT = cv.take([128, 4, 128], BF16)
            sbias = cv.take([128, 2, 128], F32)
            vg = [cv.take([128, 4, 64], F32) for _ in range(2)]
            vc = [cv.take([128, 4, 64], F32) for _ in range(2)]
            vsq = cv.take([128, 4, 64], F32)
            st = [cv.take([128, 8], F32) for _ in range(2)]
            vnpad = [cv.take([128, 4, 128], BF16) for _ in range(2)]
            uT = [cv.take([128, 2, 512], F32) for _ in range(2)]
            yct = cv.take([128, 2, 128], F32)
            ycb = [cv.take([128, 2, 512], BF16) for _ in range(2)]
            b_swn, b_swT, b_sbias, b_vsq, b_yct = Buf("swn"), Buf("swT"), Buf("sbias"), Buf("vsq"), Buf("yct")
            b_vg = [Buf("vg0"), Buf("vg1")]
            b_vc = [Buf("vc0"), Buf("vc1")]
            b_st = [Buf("st0"), Buf("st1")]
            b_vn = [Buf("vn0"), Buf("vn1")]
            b_uT = [Buf("uT0"), Buf("uT1")]
            b_ycb = [Buf("ycb0"), Buf("ycb1")]
            g.op("pool", lambda e: e.dma_start(out=swn, in_=sgu_w[l].rearrange("g p q -> p g q")), writes=[b_swn], dma=True)
            bk, bkT, b_bk = next_bank()
            for gi in range(4):
                g.op("pe", lambda e, bkT=bkT, gi=gi: e.transpose(out=bkT[:, gi * 128:(gi + 1) * 128], in_=swn[:, gi, :], identity=ident[:]),
                     reads=[b_swn, b_ident], writes=[b_bk])
            g.op("act", lambda e, bkT=bkT: e.copy(out=swT, in_=bkT[:, 0:512].rearrange("p (g q) -> p g q", g=4)), reads=[b_bk], writes=[b_swT])
            for gi in range(4):
                g.op("sp", lambda e, gi=gi: e.dma_start(out=sbias[(gi % 2) * 64:(gi % 2) * 64 + 64, gi // 2, :],
                                                      in_=sgu_b[l, gi:gi + 1, :].partition_broadcast(64)), writes=[b_sbias], dma=True, multi=(gi != 0))
            for i in range(2):
                g.op("pool", lambda e, i=i: e.memset(vnpad[i], 0.0), writes=[b_vn[i]])

        def src_ap(s_, t0):
            return (x_in if src_is_input else xres)[s_, t0:t0 + 128, :]

        src_bufs = [] if src_is_input else [b_xres[s]]
        stores = []

        def rope_evac(bk, b_bk, dst, b_dst, rpk, b_rpk):
            bv = bk[:].rearrange("p (h d) -> p h d", h=8)
            cosb = rpk[:, 0:8].unsqueeze(1).to_broadcast([128, 8, 8])
            sinb = rpk[:, 8:16].unsqueeze(1).to_broadcast([128, 8, 8])
            x1, x2 = bv[:, :, 0:8], bv[:, :, 8:16]
            g.op("dve", lambda e: e.tensor_tensor(out=rt[0], in0=x1, in1=cosb, op=ALU.mult), reads=[b_bk, b_rpk], writes=[b_rt[0]])
            g.op("dve", lambda e: e.tensor_tensor(out=rt[1], in0=x2, in1=sinb, op=ALU.mult), reads=[b_bk, b_rpk], writes=[b_rt[1]])
            g.op("dve", lambda e: e.tensor_tensor(out=rt[2], in0=x2, in1=cosb, op=ALU.mult), reads=[b_bk, b_rpk], writes=[b_rt[2]])
            g.op("dve", lambda e: e.tensor_tensor(out=rt[3], in0=x1, in1=sinb, op=ALU.mult), reads=[b_bk, b_rpk], writes=[b_rt[3]])
            g.op("dve", lambda e: e.tensor_copy(out=dst[:, :, 16:64], in_=bv[:, :, 16:64]), reads=[b_bk], writes=[b_dst])
            g.op("pool", lambda e: e.tensor_tensor(out=dst[:, :, 0:8], in0=rt[0], in1=rt[1], op=ALU.subtract),
                 reads=[b_rt[0], b_rt[1]], writes=[b_dst])
            g.op("pool", lambda e: e.tensor_tensor(out=dst[:, :, 8:16], in0=rt[2], in1=rt[3], op=ALU.add),
                 reads=[b_rt[2], b_rt[3]], writes=[b_dst])

        def pool_elem(tc):
            NCH = S // 512
            if True:
                kz = tc % 2
                g.op("sp", lambda e, kz=kz, tc=tc: e.dma_start(out=zch[kz], in_=zaD[s, :, :, tc * 512:tc * 512 + 528]),
                     reads=[b_zaD[s]], writes=[b_zch[kz]], dma=True)
                b_zaT = b_zch[kz]
                Z = lambda off, n, kz=kz: zch[kz][:, :, 8 + off:8 + off + n]
                g.op("pool", lambda e, Z=Z: e.tensor_tensor(out=T1[:, :, 1:528], in0=Z(-7, 527), in1=Z(-8, 527), op=ALU.add), reads=[b_zaT], writes=[b_T1])
                g.op("pool", lambda e: e.tensor_tensor(out=T2[:, :, 3:528], in0=T1[:, :, 3:528], in1=T1[:, :, 1:526], op=ALU.add), reads=[b_T1], writes=[b_T2])
                g.op("pool", lambda e: e.tensor_tensor(out=T3[:, :, 7:528], in0=T2[:, :, 7:528], in1=T2[:, :, 3:524], op=ALU.add), reads=[b_T2], writes=[b_T3])
                g.op("pool", lambda e: e.tensor_copy(out=Sb[0:64, 0, :], in_=T1[0:64, 0, 8:520]), reads=[b_T1], writes=[b_Sb])
                g.op("pool", lambda e: e.tensor_copy(out=Sb[64:128, 0, :], in_=T2[64:128, 0, 9:521]), reads=[b_T2], writes=[b_Sb], multi=True)
                g.op("pool", lambda e: e.tensor_copy(out=Sb[0:64, 1, :], in_=T3[0:64, 1, 11:523]), reads=[b_T3], writes=[b_Sb], multi=True)
                g.op("pool", lambda e: e.tensor_tensor(out=Sb[64:128, 1, :], in0=T3[64:128, 1, 15:527], in1=T3[64:128, 1, 7:519], op=ALU.add),
                     reads=[b_T3], writes=[b_Sb], multi=True)
                for cc in range(2):
                    g.op("pool", lambda e, cc=cc: e.tensor_tensor(out=T1[:, cc, 0:512], in0=Sb[:, cc, :], in1=piw[:, cc:cc + 1].to_broadcast([128, 512]), op=ALU.mult),
                         reads=[b_Sb, b_piw], writes=[b_T1], multi=(cc != 0))
                    g.op("pool", lambda e, cc=cc, Z=Z: e.tensor_tensor(out=pbf[:, cc, :], in0=T1[:, cc, 0:512], in1=Z(0, 512)[:, cc, :], op=ALU.subtract),
                         reads=[b_T1, b_zaT], writes=[b_pbf], multi=(cc != 0))
                for edge, cols in ((0, slice(0, 8)), (1, slice(504, 512))):
                    if (edge == 0 and tc == 0) or (edge == 1 and tc == NCH - 1):
                        off = 0 if edge == 0 else 504
                        g.op("pool", lambda e, edge=edge, cols=cols: e.tensor_tensor(out=etmp, in0=Sb[:, :, cols], in1=pic[:, :, edge, :], op=ALU.mult),
                             reads=[b_Sb, b_pic], writes=[b_etmp])
                        g.op("pool", lambda e, cols=cols, Z=Z, off=off: e.tensor_tensor(out=pbf[:, :, cols], in0=etmp, in1=Z(off, 8), op=ALU.subtract),
                             reads=[b_etmp, b_zaT, b_pbf], writes=[b_pbf])
        def pool_mm(tc):
            if True:
                ky = tc % 2
                for cc in range(2):
                    bk, _, b_bk = next_bank()
                    g.op("pe", lambda e, bk=bk, cc=cc: e.matmul(bk[:], lhsT=pwbd[:, cc, :], rhs=pbf[:, cc, :], start=True, stop=True),
                         reads=[b_pw, b_pbf], writes=[b_bk])
                    g.op("act", lambda e, bk=bk, cc=cc, ky=ky: e.activation(out=yab[ky][:, cc, :], in_=bk[:], func=AF.Copy, scale=psc[:, cc:cc + 1]),
                         reads=[b_bk, b_psc], writes=[b_yab[ky]], multi=(cc != 0))
                stores.append(g.op("sp", lambda e, ky=ky, tc=tc: e.dma_start(
                    out=mixA[s, :, tc * 512:(tc + 1) * 512].rearrange("(c p) t -> p c t", p=128), in_=yab[ky]),
                    reads=[b_yab[ky]], writes=[b_mixA[s]], dma=True, multi=True))

        def sgu_part2(m, k, j, tg):
            cbk, _, b_cbk = next_bank()
            for pr in range(2):
                for gh in range(2):
                    gi = 2 * pr + gh
                    g.op("pe", lambda e, cbk=cbk, pr=pr, gh=gh, gi=gi, m=m: e.matmul(
                        cbk[:, pr * 128:(pr + 1) * 128], lhsT=vnpad[m][:, gi, :], rhs=swT[:, gi, :], start=(gh == 0), stop=(gh == 1)),
                        reads=[b_vn[m], b_swT], writes=[b_cbk])
            g.op("dve", lambda e, cbk=cbk: e.tensor_tensor(out=yct, in0=cbk[:, 0:256].rearrange("p (a q) -> p a q", a=2), in1=sbias, op=ALU.add),
                 reads=[b_cbk, b_sbias], writes=[b_yct])
            g.op("dve", lambda e, k=k, j=j: e.tensor_tensor(out=ycb[k][:, :, j * 128:(j + 1) * 128], in0=yct, in1=uT[k][:, :, j * 128:(j + 1) * 128], op=ALU.mult),
                 reads=[b_yct, b_uT[k]], writes=[b_ycb[k]], multi=(j != 0))
            if j == 3:
                stores.append(g.op("sp", lambda e, k=k, tg=tg: e.dma_start(
                    out=mixC[s, :, tg * 512:(tg + 1) * 512].rearrange("(c p) t -> p c t", p=128), in_=ycb[k]),
                    reads=[b_ycb[k]], writes=[b_mixC[s]], dma=True, multi=True))

        sgu_pend = [None]
        NTG1 = S // 512
        norm_a(src_ap, s, 0, 512, src_bufs, hbs, b_hbs)
        norm_b(512, hT[0], b_hT[0], hbs, b_hbs)
        for tg in range(NTG1):
            k = tg % 2
            if full:
                for cc in range(2):
                    bk, _, b_bk = next_bank()
                    for kc in range(8):
                        g.op("pe", lambda e, bk=bk, kc=kc, cc=cc, k=k: e.matmul(
                            bk[:], lhsT=win[:, kc, cc * 128:(cc + 1) * 128], rhs=hT[k][:, kc, :], start=(kc == 0), stop=(kc == 7)),
                            reads=[b_win, b_hT[k]], writes=[b_bk])
                    g.op("dve", lambda e, bk=bk, cc=cc, k=k: e.tensor_copy(out=zst[k][:, cc, :], in_=bk[:]),
                         reads=[b_bk], writes=[b_zst[k]], multi=(cc != 0))
                g.op("sp", lambda e, k=k, tg=tg: e.dma_start(out=zaD[s, :, :, 8 + tg * 512:8 + (tg + 1) * 512], in_=zst[k]),
                     reads=[b_zst[k]], writes=[b_zaD[s]], dma=True, multi=True)
                for cc in range(2):
                    bk, _, b_bk = next_bank()
                    for kc in range(8):
                        g.op("pe", lambda e, bk=bk, kc=kc, cc=cc, k=k: e.matmul(
                            bk[:], lhsT=win[:, kc, ZCU + cc * 128:ZCU + (cc + 1) * 128], rhs=hT[k][:, kc, :], start=(kc == 0), stop=(kc == 7)),
                            reads=[b_win, b_hT[k]], writes=[b_bk])
                    g.op("act", lambda e, bk=bk, cc=cc, k=k: e.activation(out=uT[k][:, cc, :], in_=bk[:], func=AF.Gelu_apprx_tanh),
                         reads=[b_bk], writes=[b_uT[k]], multi=(cc != 0))
            if tg + 1 < NTG1:
                norm_a(src_ap, s, tg + 1, 512, src_bufs, hbs, b_hbs)
            for j in range(4):
                t0 = tg * 512 + j * 128
                m = j % 2
                if full:
                    bk, _, b_bk = next_bank()
                    for kc in range(8):
                        g.op("pe", lambda e, bk=bk, kc=kc, j=j, k=k: e.matmul(
                            bk[:, 0:256], lhsT=hT[k][:, kc, j * 128:(j + 1) * 128], rhs=win[:, kc, ZCV:ZCV + 256],
                            start=(kc == 0), stop=(kc == 7)), reads=[b_hT[k], b_win], writes=[b_bk])
                    g.op("act", lambda e, bk=bk, m=m: e.activation(out=vg[m].rearrange("p g c -> p (g c)"), in_=bk[:, 0:256], func=AF.Gelu_apprx_tanh),
                         reads=[b_bk], writes=[b_vg[m]])
                    g.op("dve", lambda e, m=m: e.tensor_reduce(out=st[m][:, 0:4], in_=vg[m], axis=mybir.AxisListType.X, op=ALU.add),
                         reads=[b_vg[m]], writes=[b_st[m]])
                    g.op("dve", lambda e, m=m: e.tensor_scalar(out=st[m][:, 0:4], in0=st[m][:, 0:4], scalar1=1.0 / 64, scalar2=None, op0=ALU.mult),
                         reads=[b_st[m]], writes=[b_st[m]])
                    g.op("dve", lambda e, m=m: e.tensor_tensor(out=vc[m], in0=vg[m], in1=st[m][:, 0:4].unsqueeze(2).to_broadcast([128, 4, 64]), op=ALU.subtract),
                         reads=[b_vg[m], b_st[m]], writes=[b_vc[m]])
                    g.op("dve", lambda e, m=m: e.tensor_tensor(out=vsq, in0=vc[m], in1=vc[m], op=ALU.mult), reads=[b_vc[m]], writes=[b_vsq])
                    g.op("dve", lambda e, m=m: e.tensor_reduce(out=st[m][:, 4:8], in_=vsq, axis=mybir.AxisListType.X, op=ALU.add),
                         reads=[b_vsq], writes=[b_st[m]])
                    g.op("dve", lambda e, m=m: e.tensor_scalar(out=st[m][:, 4:8], in0=st[m][:, 4:8], scalar1=1.0 / 64, scalar2=EPS, op0=ALU.mult, op1=ALU.add),
                         reads=[b_st[m]], writes=[b_st[m]])
                    g.op("act", lambda e, m=m: e.activation(out=st[m][:, 4:8], in_=st[m][:, 4:8], func=AF.Sqrt), reads=[b_st[m]], writes=[b_st[m]])
                    g.op("dve", lambda e, m=m: e.reciprocal(out=st[m][:, 4:8], in_=st[m][:, 4:8]), reads=[b_st[m]], writes=[b_st[m]])
                    for gi in range(4):
                        g.op("dve", lambda e, m=m, gi=gi: e.tensor_scalar(
                            out=vnpad[m][:, gi, (gi % 2) * 64:(gi % 2) * 64 + 64], in0=vc[m][:, gi, :], scalar1=st[m][:, 4 + gi:5 + gi], scalar2=None, op0=ALU.mult),
                            reads=[b_vc[m], b_st[m]], writes=[b_vn[m]], multi=(gi != 0))
                if use_rope or rope_mode == "loads":
                    g.op("sp", lambda e, t0=t0, m=m: e.dma_start(out=rp[m], in_=rope_in[t0:t0 + 128, :]), writes=[b_rp[m]], dma=True)
                for name, col0 in (("q", QO), ("k", KO), ("v", VO)):
                    bk, _, b_bk = next_bank()
                    for kc in range(8):
                        g.op("pe", lambda e, bk=bk, kc=kc, j=j, k=k, col0=col0: e.matmul(
                            bk[:], lhsT=hT[k][:, kc, j * 128:(j + 1) * 128], rhs=win[:, kc, col0:col0 + 512],
                            start=(kc == 0), stop=(kc == 7)), reads=[b_hT[k], b_win], writes=[b_bk])
                    if not use_rope and name in ("q", "k"):
                        dst_, b_dst_ = (qb[m], b_qb[m]) if name == "q" else (kb[m], b_kb[m])
                        g.op("act", lambda e, bk=bk, dst_=dst_: e.copy(out=dst_.rearrange("p h d -> p (h d)"), in_=bk[:]),
                             reads=[b_bk], writes=[b_dst_])
                    elif name == "q":
                        rope_evac(bk, b_bk, qb[m], b_qb[m], rp[m], b_rp[m])
                    elif name == "k":
                        rope_evac(bk, b_bk, kb[m], b_kb[m], rp[m], b_rp[m])
                    if name == "q":
                        stores.append(g.op("sp", lambda e, t0=t0, m=m: e.dma_start(
                            out=qd[s, t0:t0 + 128, :], in_=qb[m].rearrange("p h d -> p (h d)")),
                            reads=[b_qb[m]], writes=[b_qd[s]], dma=True, multi=True))
                    elif name == "k":
                        stores.append(g.op("sp", lambda e, t0=t0, m=m: e.dma_start(
                            out=kd[s, t0:t0 + 128, :], in_=kb[m].rearrange("p h d -> p (h d)")),
                            reads=[b_kb[m]], writes=[b_kd[s]], dma=True, multi=True))
                    else:
                        g.op("act", lambda e, bk=bk, m=m: e.copy(out=vb[m], in_=bk[:]), reads=[b_bk], writes=[b_vb[m]])
                        stores.append(g.op("sp", lambda e, t0=t0, m=m: e.dma_start(out=vd[s, t0:t0 + 128, :], in_=vb[m]),
                                           reads=[b_vb[m]], writes=[b_vd[s]], dma=True, multi=True))
                if full:
                    if sgu_pend[0] is not None:
                        sgu_part2(*sgu_pend[0])
                    sgu_pend[0] = (m, k, j, tg)
                if full and tg >= 1 and j == 1:
                    pool_elem(tg - 1)
            if full and tg >= 1:
                pool_mm(tg - 1)
            if tg + 1 < NTG1:
                norm_b(512, hT[1 - k], b_hT[1 - k], hbs, b_hbs)
        if full and sgu_pend[0] is not None:
            sgu_part2(*sgu_pend[0])
        if full:
            pool_elem(S // 512 - 1)
            pool_mm(S // 512 - 1)
        return stores


    PATTERNS = (1, 4, 16)

    def phase_attn(l, s, dump=False, outproj=False, dbg_l0=False):
        g.barrier()
        assert S % 2048 == 0
        cv = Carver("P2 attn")
        kTw = cv.take([128, 4096], BF16)
        qTb = cv.take([128, 2048], BF16)
        NVT = 17 + 20 + 32
        Vt = cv.take([128, NVT, 128], BF16)
        acc = [cv.take([128, 2, 2048], F32) for _ in range(2)]
        pT = [cv.take([128, 512], BF16) for _ in range(4)]
        msk = cv.take([128, 4, 256], BF16)
        ones = cv.take([128, 64], BF16)
        ybT = cv.take([128, 8, 2048], BF16)
        b_kTw, b_qTb, b_msk, b_ones, b_ybT = Buf("kTw"), Buf("qTb"), Buf("msk"), Buf("ones"), Buf("ybT")
        b_V = {d: Buf(f"V{d}") for d in PATTERNS}
        b_acc = [Buf("acc0"), Buf("acc1")]
        b_pT = [Buf(f"pT{i}") for i in range(4)]
        vbase = {1: 0, 4: 17, 16: 37}
        if outproj:
            woA = cv.take([128, 2, D], BF16)
            woC = cv.take([128, 2, D], BF16)
            woB = cv.take([128, 8, D], BF16)
            yaT = cv.take([128, 2, 2048], BF16)
            ycT = cv.take([128, 2, 2048], BF16)
            b_woA, b_woC, b_woB, b_yaT, b_ycT = Buf("woA"), Buf("woC"), Buf("woB"), Buf("yaT"), Buf("ycT")
            g.op("pool", lambda e: e.dma_start(out=woA, in_=w_out[l, 0:256, :].rearrange("(c p) n -> p c n", p=128)), writes=[b_woA], dma=True)
            g.op("pool", lambda e: e.dma_start(out=woC, in_=w_out[l, 768:1024, :].rearrange("(c p) n -> p c n", p=128)), writes=[b_woC], dma=True)
            g.op("pool", lambda e: e.memset(woB, 0.0), writes=[b_woB])
            g.op("pool", lambda e: e.dma_start(out=woB[0:64, :, :], in_=w_out[l, 256:768, :].rearrange("(h p) n -> p h n", p=64)),
                 writes=[b_woB], dma=True, multi=True)
            load_mod(s, l, 0, g_mix)
        g.op("sp", lambda e: e.dma_start(out=msk, in_=masks_in.ap()), writes=[b_msk], dma=True)
        g.op("sp", lambda e: e.dma_start(out=ones, in_=ones_in.ap()), writes=[b_ones], dma=True)
        g.op("pool", lambda e: e.memset(ybT, 0.0), writes=[b_ybT])
        g.op("pool", lambda e: e.memset(kTw, 0.0), writes=[b_kTw])
        g.op("pool", lambda e: e.memset(Vt, 0.0), writes=[b_V[1], b_V[4], b_V[16]])
        NSB = S // 2048
        pcount = [0]
        outs = []
        for SB in range(NSB):
            T0 = SB * 2048
            for hp in range(4):
                cs = slice(hp * 128, (hp + 1) * 128)
                w_lo, w_hi = max(T0 - 1024, 0), min(T0 + 3072, S)
                c_lo = w_lo - (T0 - 1024)
                for t in range(w_lo, w_hi, 512):
                    g.op("sp", lambda e, t=t, c0=c_lo + (t - w_lo), cs=cs: e.dma_start_transpose(
                        out=kTw[:, c0:c0 + 512], in_=kd[s, t:t + 512, cs]), reads=[b_kd[s]], writes=[b_kTw], dma=True, multi=(t != w_lo))
                for t in range(0, 2048, 512):
                    g.op("sp", lambda e, t=t, cs=cs, T0=T0: e.dma_start_transpose(
                        out=qTb[:, t:t + 512], in_=qd[s, T0 + t:T0 + t + 512, cs]), reads=[b_qd[s]], writes=[b_qTb], dma=True, multi=(t != 0))
                if w_lo > T0 - 1024:
                    g.op("pool", lambda e, c_lo=c_lo: e.memset(kTw[:, 0:c_lo], 0.0), writes=[b_kTw], multi=True)
                if w_hi < T0 + 3072:
                    g.op("pool", lambda e, c1=w_hi - (T0 - 1024): e.memset(kTw[:, c1:4096], 0.0), writes=[b_kTw], multi=True)
                for d in PATTERNS:
                    nb = 16 // d
                    Lf = S // d
                    for r in range(d):
                        for u in range(nb + 1):
                            kf0 = (SB * nb + u) * 128 - 64
                            p0, p1 = max(0, -kf0), min(128, Lf - kf0)
                            slot = vbase[d] + r * (nb + 1) + u
                            row0 = (kf0 + p0) * d + r
                            rows = slice(row0, row0 + (p1 - p0 - 1) * d + 1, d)
                            g.op("sp", lambda e, slot=slot, p0=p0, p1=p1, rows=rows, cs=cs: e.dma_start(
                                out=Vt[p0:p1, slot, :], in_=vd[s, rows, cs]), reads=[b_vd[s]], writes=[b_V[d]], dma=True,
                                multi=not (r == 0 and u == 0))
                            if p0 > 0:
                                g.op("pool", lambda e, slot=slot, p0=p0: e.memset(Vt[0:p0, slot, :], 0.0), writes=[b_V[d]], multi=True)
                            if p1 < 128:
                                g.op("pool", lambda e, slot=slot, p1=p1: e.memset(Vt[p1:128, slot, :], 0.0), writes=[b_V[d]], multi=True)
                LAG = 3
                items = []
                for pi, d in enumerate(PATTERNS):
                    nb = 16 // d
                    blocks = [(r, j) for r in range(d) for j in range(nb)]
                    for it_ in range(0, len(blocks), 2):
                        for h in range(2):
                            items.append((pi, d, blocks[it_:it_ + 2], h))

                def stage_a(item):
                    pi, d, pair, h = item
                    nb = 16 // d
                    NBf = S // (128 * d)
                    rows = slice(h * 64, (h + 1) * 64)
                    sbk, _, b_sbk = next_bank()
                    for bi, (r, j) in enumerate(pair):
                        q0 = j * 128 * d + r
                        qs = slice(q0, q0 + 127 * d + 1, d)
                        for m in range(2):
                            k0 = (j * 128 - 64 + m * 128) * d + r + 1024
                            ks = slice(k0, k0 + 127 * d + 1, d)
                            c0 = (bi * 2 + m) * 128
                            g.op("pe", lambda e, sbk=sbk, c0=c0, rows=rows, ks=ks, qs=qs: e.matmul(
                                sbk[:, c0:c0 + 128], lhsT=kTw[rows, ks], rhs=qTb[rows, qs], start=True, stop=True),
                                reads=[b_kTw, b_qTb], writes=[b_sbk])
                    pk = pcount[0] % 4
                    pcount[0] += 1
                    g.op("act", lambda e, sbk=sbk, pk=pk: e.activation(out=pT[pk], in_=sbk[:], func=AF.Exp, scale=0.125),
                         reads=[b_sbk], writes=[b_pT[pk]])
                    for bi, (r, j) in enumerate(pair):
                        ib = SB * nb + j
                        mi = (1 if ib == 0 else 0) + (2 if ib == NBf - 1 else 0)
                        eng = "pool"
                        g.op(eng, lambda e, pk=pk, bi=bi, mi=mi: e.tensor_tensor(
                            out=pT[pk][:, bi * 256:(bi + 1) * 256], in0=pT[pk][:, bi * 256:(bi + 1) * 256], in1=msk[:, mi, :], op=ALU.mult),
                            reads=[b_pT[pk], b_msk], writes=[b_pT[pk]])
                    return pk

                def stage_b(item, pk):
                    pi, d, pair, h = item
                    nb = 16 // d
                    rows = slice(h * 64, (h + 1) * 64)
                    obk, _, b_obk = next_bank()
                    for bi, (r, j) in enumerate(pair):
                        for m in range(2):
                            slot = vbase[d] + r * (nb + 1) + j + m
                            pc = (bi * 2 + m) * 128
                            g.op("pe", lambda e, obk=obk, bi=bi, m=m, slot=slot, pk=pk, pc=pc, rows=rows: e.matmul(
                                obk[0:64, bi * 128:(bi + 1) * 128], lhsT=Vt[:, slot, rows], rhs=pT[pk][:, pc:pc + 128],
                                start=(m == 0), stop=(m == 1)), reads=[b_V[d], b_pT[pk]], writes=[b_obk])
                        for m in range(2):
                            pc = (bi * 2 + m) * 128
                            g.op("pe", lambda e, obk=obk, bi=bi, m=m, pk=pk, pc=pc: e.matmul(
                                obk[0:64, 256 + bi * 128:256 + (bi + 1) * 128], lhsT=ones, rhs=pT[pk][:, pc:pc + 128],
                                start=(m == 0), stop=(m == 1)), reads=[b_ones, b_pT[pk]], writes=[b_obk])
                    (r0, j0), (r1, j1) = pair
                    if r0 == r1:
                        a0 = j0 * 128 * d + r0
                        pieces = [(acc[h][0:64, :, a0:a0 + 255 * d + 1:d], obk[0:64, :].rearrange("p (n q) -> p n q", n=2))]
                    else:
                        pieces = []
                        for bi, (r, j) in enumerate(pair):
                            ab = j * 128 * d + r
                            pieces.append((acc[h][0:64, :, ab:ab + 127 * d + 1:d],
                                           obk[0:64, :].rearrange("p (n b q) -> p n b q", n=2, b=2)[:, :, bi, :]))
                    for dst, src in pieces:
                        if pi == 0:
                            g.op("act", lambda e, dst=dst, src=src: e.copy(out=dst, in_=src), reads=[b_obk], writes=[b_acc[h]])
                        else:
                            g.op("dve", lambda e, dst=dst, src=src: e.tensor_tensor(out=dst, in0=dst, in1=src, op=ALU.add),
                                 reads=[b_obk, b_acc[h]], writes=[b_acc[h]])

                pend = []
                for n in range(len(items) + LAG):
                    if n < len(items):
                        pend.append((items[n], stage_a(items[n])))
                    if n >= LAG:
                        stage_b(*pend[n - LAG])
                for h in range(2):
                    hh = hp * 2 + h
                    g.op("dve", lambda e, h=h: e.reciprocal(out=acc[h][0:64, 1, :], in_=acc[h][0:64, 1, :]), reads=[b_acc[h]], writes=[b_acc[h]])
                    g.op("dve", lambda e, h=h, hh=hh: e.tensor_tensor(out=ybT[0:64, hh, :], in0=acc[h][0:64, 0, :], in1=acc[h][0:64, 1, :], op=ALU.mult),
                         reads=[b_acc[h]], writes=[b_ybT], multi=True)
                    if dump:
                        outs.append(g.op("sp", lambda e, hh=hh, T0=T0: e.dma_start(out=dyb[s, hh, :, T0:T0 + 2048], in_=ybT[0:64, hh, :]),
                                         reads=[b_ybT], dma=True))
            if outproj:
                g.op("sp", lambda e, T0=T0: e.dma_start(out=yaT, in_=mixA[s, :, T0:T0 + 2048].rearrange("(c p) t -> p c t", p=128)),
                     reads=[b_mixA[s]], writes=[b_yaT], dma=True)
                g.op("sp", lambda e, T0=T0: e.dma_start(out=ycT, in_=mixC[s, :, T0:T0 + 2048].rearrange("(c p) t -> p c t", p=128)),
                     reads=[b_mixC[s]], writes=[b_ycT], dma=True)
                xsrc = x_in if l == 0 else xres
                for tt in range(16):
                    t0 = T0 + tt * 128
                    tsl = slice(tt * 128, (tt + 1) * 128)
                    xi = cnt["tt"] % NXB
                    cnt["tt"] += 1
                    g.op("sp", lambda e, xi=xi, t0=t0, xsrc=xsrc: e.dma_start(out=xt[xi][:], in_=xsrc[s, t0:t0 + 128, :]),
                         reads=([] if l == 0 else [b_xres[s]]), writes=[b_xt[xi]], dma=True)
                    for half in range(2):
                        hs = slice(half * 512, (half + 1) * 512)
                        bk, _, b_bk = next_bank()
                        ops_ = [(yaT[:, c, tsl], woA[:, c, hs], b_yaT, b_woA) for c in range(2)] + \
                               [(ybT[:, h_, tsl], woB[:, h_, hs], b_ybT, b_woB) for h_ in range(8)] + \
                               [(ycT[:, c, tsl], woC[:, c, hs], b_ycT, b_woC) for c in range(2)]
                        for i_, (lt, rh, bl, br) in enumerate(ops_):
                            g.op("pe", lambda e, bk=bk, lt=lt, rh=rh, i_=i_: e.matmul(bk[:], lhsT=lt, rhs=rh, start=(i_ == 0), stop=(i_ == 11)),
                                 reads=[bl, br], writes=[b_bk])
                        g.op("dve", lambda e, bk=bk, hs=hs: e.tensor_tensor(out=yo1[:, hs], in0=bk[:], in1=GTm[:, hs], op=ALU.mult),
                             reads=[b_bk, b_GT], writes=[b_yo1])
                    g.op("pool", lambda e, xi=xi: e.tensor_tensor(out=yo1[:], in0=yo1[:], in1=xt[xi][:], op=ALU.add),
                         reads=[b_yo1, b_xt[xi]], writes=[b_yo1])
                    st_ = g.op("sp", lambda e, t0=t0: e.dma_start(out=xres[s, t0:t0 + 128, :], in_=yo1[:]),
                               reads=[b_yo1], writes=[b_xres[s]], dma=True, multi=True)
                    if dbg_l0:
                        outs.append(g.op("sp", lambda e, t0=t0: e.dma_start(out=x1dbg[s, t0:t0 + 128, :], in_=yo1[:]), reads=[b_yo1], dma=True))
        return outs

    def phase_mlp(l, s, src_is_input, last):
        g.barrier()
        TG = 256
        NT = TG // 128
        cv = Carver("P3 mlp")
        wup = cv.take([128, 8, DFF], BF16)
        wdn = cv.take([128, 32, D], BF16)
        uT = cv.take([128, 32, TG], BF16)
        hT = [cv.take([128, 8, TG], BF16) for _ in range(2)]
        rl = [cv.take([128, TG], F32) for _ in range(4)]
        b_wup, b_wdn = Buf("wup"), Buf("wdn")
        b_uT = [Buf(f"uT{i}") for i in range(32)]
        b_hT = [Buf("hT0"), Buf("hT1")]
        b_rl = [Buf(f"rl{i}") for i in range(4)]
        g.op("pool", lambda e: e.dma_start(out=wup, in_=w_up[l].rearrange("(kc p) n -> p kc n", p=128)), writes=[b_wup], dma=True)
        g.op("pool", lambda e: e.dma_start(out=wdn, in_=w_down[l].rearrange("(fc p) n -> p fc n", p=128)), writes=[b_wdn], dma=True)
        load_mod(s, l, 1, g_mlp)

        def src_ap(s_, t0):
            return (x_in if src_is_input else xres)[s_, t0:t0 + 128, :]

        src_bufs = [] if src_is_input else [b_xres[s]]
        hbs3 = [hb[0][:], hb[1][:]]
        NTG3 = S // TG
        norm_a(src_ap, s, 0, TG, src_bufs, hbs3, b_hb)
        norm_b(TG, hT[0], b_hT[0], hbs3, b_hb)
        for tg in range(NTG3):
            k = tg % 2
            for fc in range(32):
                bk, _, b_bk = next_bank()
                for kc in range(8):
                    g.op("pe", lambda e, bk=bk, kc=kc, fc=fc, k=k: e.matmul(
                        bk[:, 0:TG], lhsT=wup[:, kc, fc * 128:(fc + 1) * 128], rhs=hT[k][:, kc, :], start=(kc == 0), stop=(kc == 7)),
                        reads=[b_wup, b_hT[k]], writes=[b_bk])
                m = fc % 4
                if fc % 2 == 0:
                    g.op("act", lambda e, bk=bk, m=m: e.activation(out=rl[m], in_=bk[:, 0:TG], func=AF.Relu),
                         reads=[b_bk], writes=[b_rl[m]])
                    g.op("pool", lambda e, fc=fc, m=m: e.tensor_tensor(out=uT[:, fc, :], in0=rl[m], in1=rl[m], op=ALU.mult),
                         reads=[b_rl[m]], writes=[b_uT[fc]])
                else:
                    g.op("dve", lambda e, bk=bk, m=m: e.tensor_scalar(out=rl[m], in0=bk[:, 0:TG], scalar1=0.0, scalar2=None, op0=ALU.max),
                         reads=[b_bk], writes=[b_rl[m]])
                    g.op("dve", lambda e, fc=fc, m=m: e.tensor_tensor(out=uT[:, fc, :], in0=rl[m], in1=rl[m], op=ALU.mult),
                         reads=[b_rl[m]], writes=[b_uT[fc]])
            if tg + 1 < NTG3:
                norm_a(src_ap, s, tg + 1, TG, src_bufs, hbs3, b_hb)
            for j in range(NT):
                t0 = tg * TG + j * 128
                xi = cnt["tt"] % NXB
                cnt["tt"] += 1
                g.op("sp", lambda e, xi=xi, t0=t0: e.dma_start(out=xt[xi][:], in_=src_ap(s, t0)),
                     reads=src_bufs, writes=[b_xt[xi]], dma=True)
                for half in range(2):
                    bk, _, b_bk = next_bank()
                    for fc in range(32):
                        g.op("pe", lambda e, bk=bk, fc=fc, j=j, half=half: e.matmul(
                            bk[:], lhsT=uT[:, fc, j * 128:(j + 1) * 128], rhs=wdn[:, fc, half * 512:(half + 1) * 512],
                            start=(fc == 0), stop=(fc == 31)), reads=[b_uT[fc], b_wdn], writes=[b_bk])
                    hs = slice(half * 512, (half + 1) * 512)
                    g.op("dve", lambda e, bk=bk, hs=hs: e.tensor_tensor(out=yo1[:, hs], in0=bk[:], in1=GTm[:, hs], op=ALU.mult),
                         reads=[b_bk, b_GT], writes=[b_yo1])
                g.op("pool", lambda e, xi=xi: e.tensor_tensor(out=yo1[:], in0=yo1[:], in1=xt[xi][:], op=ALU.add),
                     reads=[b_yo1, b_xt[xi]], writes=[b_yo1])
                if not last:
                    g.op("sp", lambda e, t0=t0: e.dma_start(out=xres[s, t0:t0 + 128, :], in_=yo1[:]),
                         reads=[b_yo1], writes=[b_xres[s]], dma=True, multi=True)
                else:
                    m = j % 2
                    g.op("dve", lambda e, m=m: e.memset(fs[m][:], 0.0), writes=[b_fs[m]])
                    g.op("act", lambda e, m=m: e.activation(out=junk[:], in_=yo1[:], func=AF.Square, accum_out=fs[m][:, 0:1]),
                         reads=[b_yo1], writes=[b_junk, b_fs[m]])
                    g.op("dve", lambda e, m=m: e.tensor_scalar(out=fs[m][:, 1:2], in0=fs[m][:, 0:1], scalar1=1.0 / D, scalar2=EPS,
                                                               op0=ALU.mult, op1=ALU.add), reads=[b_fs[m]], writes=[b_fs[m]])
                    g.op("act", lambda e, m=m: e.activation(out=fs[m][:, 1:2], in_=fs[m][:, 1:2], func=AF.Sqrt), reads=[b_fs[m]], writes=[b_fs[m]])
                    g.op("dve", lambda e, m=m: e.reciprocal(out=fs[m][:, 1:2], in_=fs[m][:, 1:2]), reads=[b_fs[m]], writes=[b_fs[m]])
                    g.op("dve", lambda e, m=m: e.scalar_tensor_tensor(out=yo1[:], in0=yo1[:], scalar=fs[m][:, 1:2], in1=gf_bc[:],
                                                                      op0=ALU.mult, op1=ALU.mult), reads=[b_yo1, b_fs[m], b_gf], writes=[b_yo1])
                    finals.append(g.op("sp", lambda e, t0=t0: e.dma_start(out=y_out[s, t0:t0 + 128, :], in_=yo1[:]),
                                       reads=[b_yo1], dma=True))
            if tg + 1 < NTG3:
                norm_b(TG, hT[1 - k], b_hT[1 - k], hbs3, b_hb)

    if stage == "full":
        for l in range(DEPTH):
            for s in range(nseg):
                r1 = phase_inproj(l, s, src_is_input=(l == 0))
                r2 = phase_attn(l, s, dump=(debug and l == 0), outproj=True, dbg_l0=(debug and l == 0))
                if debug and l == 0:
                    finals.extend(r1)
                    finals.extend(r2)
                phase_mlp(l, s, src_is_input=False, last=(l == DEPTH - 1))
    elif stage == "attn":
        for s in range(nseg):
            finals.extend(phase_attn(0, s, dump=True))
    elif stage.startswith("inproj"):
        mode = {"inproj": "full", "inproj_nr": "none", "inproj_loads": "loads"}[stage]
        for s in range(nseg):
            finals.extend(phase_inproj(0, s, True, rope_mode=mode))
    else:
        for l in range(DEPTH):
            for s in range(nseg):
                phase_mlp(l, s, src_is_input=(l == 0), last=(l == DEPTH - 1))
    stats = g.emit(final_wait_ops=finals)
    stats["arena_KiB"] = {k: round(v / 1024, 1) for k, v in hiwater.items()}
    stats["arena_cap_KiB"] = ARENA / 1024
    return nc, stats


def _ident():
    import ml_dtypes
    return np.eye(128, dtype=np.float32).astype(ml_dtypes.bfloat16)


def _masks():
    import ml_dtypes
    kk = np.arange(128)[:, None]
    qq = np.arange(128)[None, :]
    m0, m1 = (kk >= qq), (kk <= qq)
    out = np.zeros((128, 4, 256), np.float32)
    for first in (0, 1):
        for last in (0, 1):
            a = m0 & (kk >= 64) if first else m0
            b = m1 & (kk < 64) if last else m1
            out[:, first + 2 * last, :128] = a
            out[:, first + 2 * last, 128:] = b
    return out.astype(ml_dtypes.bfloat16)


def _pool_consts(S):
    wins = (2, 4, 8, 16)
    iw = np.zeros((128, 2), np.float32)
    ic = np.zeros((128, 2, 2, 8), np.float32)
    for c in range(2):
        for half in range(2):
            w = wins[2 * c + half]
            rows = slice(half * 64, half * 64 + 64)
            iw[rows, c] = 1.0 / w
            for i in range(8):
                t = i
                ic[rows, c, 0, i] = 1.0 / (min(t + w // 2, S) - max(t - w // 2, 0))
                t = S - 8 + i
                ic[rows, c, 1, i] = 1.0 / (min(t + w // 2, S) - max(t - w // 2, 0))
    return iw, ic


def _ones64():
    import ml_dtypes
    return np.ones((128, 64), np.float32).astype(ml_dtypes.bfloat16)


def _rope(S):
    inv = (500000.0 ** (-np.arange(0, 16, 2, dtype=np.float32) / 16.0)).astype(np.float32)
    ang = np.arange(S, dtype=np.float32)[:, None] * inv[None, :]
    return np.ascontiguousarray(np.concatenate([np.cos(ang), np.sin(ang)], axis=1).astype(np.float32))


def host_consts(S):
    iw, ic = _pool_consts(S)
    return {"ident": _ident(), "rope": _rope(S), "masks": _masks(), "ones64": _ones64(), "pool_iw": iw, "pool_ic": ic}


def kernel(x_prompt, x_sample, c_prompt, c_sample, w_ada, b_ada, g_mix, g_mlp, w_in, pool_w, pool_scale,
           sgu_w, sgu_b, w_out, w_up, w_down, g_final):
    S = 8192
    nc, _ = build(S, 2)
    f = lambda a: np.ascontiguousarray(np.asarray(a, dtype=np.float32))
    x_prompt, x_sample, c_prompt, c_sample = f(x_prompt), f(x_sample), f(c_prompt), f(c_sample)
    shared = host_consts(S)
    shared.update({"w_ada": f(w_ada), "b_ada": f(b_ada), "g_mix": f(g_mix), "g_mlp": f(g_mlp), "w_in": f(w_in),
                   "w_up": f(w_up), "w_down": f(w_down), "g_final": f(g_final).reshape(1, D),
                   "pool_w": f(pool_w), "pool_scale": np.ascontiguousarray(f(pool_scale).reshape(DEPTH, 2, 128).transpose(0, 2, 1)),
                   "sgu_w": f(sgu_w), "sgu_b": f(sgu_b), "w_out": f(w_out)})
    in_maps = []
    for i in range(NCORES):
        sq, j = i // 4, i % 4
        x = np.stack([x_sample[i], x_prompt[sq, WIN_START[j]:WIN_START[j] + S]])
        c = np.stack([c_sample[i], c_prompt[sq]]).reshape(2, 8, 128).transpose(0, 2, 1)
        in_maps.append({"x": np.ascontiguousarray(x), "c": np.ascontiguousarray(c), **shared})
    res = run_bass_kernel_spmd(nc, in_maps, core_ids=list(range(NCORES)))
    y_sample = np.stack([res.results[i]["y"][0] for i in range(NCORES)])
    y_prompt = np.empty_like(x_prompt)
    for i in range(NCORES):
        sq, j = i // 4, i % 4
        y_prompt[sq, j * 4096:(j + 1) * 4096] = res.results[i]["y"][1][WIN_OWN[j]:WIN_OWN[j] + 4096]
    return (y_prompt, y_sample)
```

```python
import numpy as np
import concourse.bass as bass
import concourse.mybir as mybir
from concourse.bass_utils import run_bass_kernel_spmd

F32 = mybir.dt.float32
BF16 = mybir.dt.bfloat16
ALU = mybir.AluOpType
AF = mybir.ActivationFunctionType

D = 1024
DFF = 4096
DEPTH = 2
EPS = 1e-6
NCORES = 8
WIN_START = (0, 2048, 6144, 8192)
WIN_OWN = (0, 2048, 2048, 4096)


class Buf:
    __slots__ = ("name", "writers", "rc", "rd", "carry")

    def __init__(self, name):
        self.name = name
        self.writers = []
        self.rc = {}
        self.rd = []
        self.carry = []


class Op:
    __slots__ = ("eng", "fn", "deps", "sig", "is_dma", "sem", "val", "k", "cells")

    def __init__(self, eng, fn, is_dma):
        self.eng = eng
        self.fn = fn
        self.deps = []
        self.sig = False
        self.is_dma = is_dma
        self.sem = None
        self.val = 0


class Graph:
    def __init__(self, nc, n_dma_sems=24):
        self.nc = nc
        self.ops = []
        self.streams = {"pe": [], "act": [], "dve": [], "pool": [], "sp": []}
        self.n_dma_sems = n_dma_sems
        self.n_sw_sems = 6
        self.rr = {"hw": 0, "sw": 0}
        self.dlast = [None] * (n_dma_sems + self.n_sw_sems)
        self.pending = {e: [] for e in self.streams}

    def barrier(self):
        deps = []
        for e in ("pe", "act", "dve", "pool"):
            comp = [o for o in self.streams[e] if not o.is_dma]
            if comp:
                comp[-1].sig = True
                deps.append(comp[-1])
        deps += [d for d in self.dlast if d is not None]
        for e in self.streams:
            self.pending[e] = list(deps)

    def op(self, eng, fn, reads=(), writes=(), dma=False, multi=False):
        o = Op(eng, fn, dma)
        cl = fn.__closure__ or ()
        o.cells = [(n, c.cell_contents) for n, c in zip(fn.__code__.co_freevars, cl)]
        deps = {}
        for b in reads:
            for w in b.writers:
                deps[id(w)] = (w, True)
        for b in writes:
            prior = list(b.rc.values()) + b.rd
            if multi:
                prior += b.carry
            else:
                prior += b.writers
                b.carry = list(b.rc.values()) + b.rd + b.writers
            for r in prior:
                if id(r) not in deps:
                    deps[id(r)] = (r, False)
        for d, raw in deps.values():
            if not d.is_dma and not dma and d.eng == eng:
                if eng == "pe" or not raw:
                    continue
            o.deps.append(d)
            d.sig = True
        if self.pending[eng]:
            for d in self.pending[eng]:
                if d.is_dma or d.eng != eng:
                    o.deps.append(d)
            self.pending[eng] = []
        if dma:
            if eng == "pool":
                k = self.n_dma_sems + self.rr["sw"] % self.n_sw_sems
                self.rr["sw"] += 1
            else:
                k = self.rr["hw"] % self.n_dma_sems
                self.rr["hw"] += 1
            if self.dlast[k] is not None:
                o.deps.append(self.dlast[k])
            o.k = k
            self.dlast[k] = o
        for b in writes:
            if multi:
                b.writers.append(o)
            else:
                b.writers = [o]
            b.rc = {}
            b.rd = []
        for b in reads:
            if o not in b.writers:
                if dma:
                    b.rd.append(o)
                else:
                    b.rc[eng] = o
        self.ops.append(o)
        self.streams[eng].append(o)
        return o

    def emit(self, final_wait_ops=()):
        nc = self.nc
        csem = {e: nc.alloc_semaphore(f"s_{e}") for e in self.streams}
        dsems = [nc.alloc_semaphore(f"s_dma{i}") for i in range(self.n_dma_sems)] + \
                [nc.alloc_semaphore(f"s_swdma{i}") for i in range(self.n_sw_sems)]
        cnt = {e: 0 for e in self.streams}
        dcnt = [0] * (self.n_dma_sems + self.n_sw_sems)
        for o in self.ops:
            if o.is_dma:
                dcnt[o.k] += 16
                o.sem, o.val = dsems[o.k], dcnt[o.k]
            elif o.sig:
                cnt[o.eng] += 1
                o.sem, o.val = csem[o.eng], cnt[o.eng]
        stats = {e: len(s) for e, s in self.streams.items()}
        stats["waits"] = 0

        def run_stream(ename, eng, extra_final=()):
            waited = {}
            for o in self.streams[ename]:
                for d in o.deps:
                    if waited.get(d.sem.num, 0) >= d.val:
                        continue
                    eng.wait_ge(d.sem, d.val)
                    waited[d.sem.num] = d.val
                    stats["waits"] += 1
                for (n, v0), c in zip(o.cells, o.fn.__closure__ or ()):
                    if c.cell_contents is not v0:
                        raise RuntimeError(f"late-bound closure variable {n!r} changed after recording "
                                           f"({o.fn.__code__.co_filename}:{o.fn.__code__.co_firstlineno}); bind it as a lambda default")
                ins = o.fn(eng)
                if o.sem is not None:
                    ins.then_inc(o.sem, 16 if o.is_dma else 1)
            for d in extra_final:
                if waited.get(d.sem.num, 0) < d.val:
                    eng.wait_ge(d.sem, d.val)
                    waited[d.sem.num] = d.val

        with nc.Block() as block:
            @block.tensor
            def _(e):
                run_stream("pe", e)

            @block.scalar
            def _(e):
                run_stream("act", e)

            @block.vector
            def _(e):
                run_stream("dve", e)

            @block.gpsimd
            def _(e):
                run_stream("pool", e)

            @block.sync
            def _(e):
                run_stream("sp", e, extra_final=final_wait_ops)
        return stats


POOL_W = 256
QO, KO, VO, ZCU, ZCV = 256, 768, 1280, 1792, 2048
PROJ = 2304


def build(seg_len=8192, nseg=2, stage="full", debug=False):
    S = seg_len
    nc = bass.Bass("TRN2", target_bir_lowering=False)
    g = Graph(nc)

    def din(name, shape, dt=F32):
        return nc.dram_tensor(name, list(shape), dt, kind="ExternalInput")

    x_in = din("x", [nseg, S, D])
    c_in = din("c", [nseg, 128, 8])
    w_ada = din("w_ada", [DEPTH, D, 6 * D])
    b_ada = din("b_ada", [DEPTH, 6 * D])
    g_mix = din("g_mix", [DEPTH, D])
    g_mlp = din("g_mlp", [DEPTH, D])
    w_in = din("w_in", [DEPTH, D, PROJ])
    w_up = din("w_up", [DEPTH, D, DFF])
    w_down = din("w_down", [DEPTH, DFF, D])
    g_final = din("g_final", [1, D])
    pool_w = din("pool_w", [DEPTH, 4, 64, 64])
    pool_sc = din("pool_scale", [DEPTH, 128, 2])
    sgu_w = din("sgu_w", [DEPTH, 4, 128, 128])
    sgu_b = din("sgu_b", [DEPTH, 4, 128])
    w_out = din("w_out", [DEPTH, D, D])
    pool_ic = din("pool_ic", [128, 2, 2, 8])
    pool_iw = din("pool_iw", [128, 2])
    ident_in = din("ident", [128, 128], BF16)
    rope_in = din("rope", [S, 16])
    y_out = nc.dram_tensor("y", [nseg, S, D], F32, kind="ExternalOutput")
    xres = nc.dram_tensor("xres", [nseg, S, D], F32, kind="Internal")
    x1dbg = nc.dram_tensor("x1dbg", [nseg, S, D], F32, kind="ExternalOutput" if (debug and stage == "full") else "Internal")
    modbc = nc.dram_tensor("modbc", [nseg, DEPTH, 128, 6 * D], F32, kind="Internal")
    dbg_kind = ("ExternalInput" if stage == "attn" else ("ExternalOutput" if (stage.startswith("inproj") or stage == "full") else "Internal")) if debug else "Internal"
    masks_in = din("masks", [128, 4, 256], BF16)
    ones_in = din("ones64", [128, 64], BF16)
    dyb = nc.dram_tensor("dyb", [nseg, 8, 64, S], BF16, kind="ExternalOutput" if (debug and stage in ("attn", "full")) else "Internal")
    qd = nc.dram_tensor("dq", [nseg, S, 512], BF16, kind=dbg_kind)
    kd = nc.dram_tensor("dk", [nseg, S, 512], BF16, kind=dbg_kind)
    vd = nc.dram_tensor("dv", [nseg, S, 512], BF16, kind=dbg_kind)
    mixA = nc.dram_tensor("mixA", [nseg, 256, S], BF16, kind="ExternalOutput" if (debug and stage == "full") else "Internal")
    mixC = nc.dram_tensor("mixC", [nseg, 256, S], BF16, kind="ExternalOutput" if (debug and stage == "full") else "Internal")
    zaD = nc.dram_tensor("zaD", [nseg, 128, 2, S + 16], F32, kind="Internal")
    b_zaD = [Buf(f"zaD{s}") for s in range(nseg)]
    b_mixA = [Buf(f"mixA{s}") for s in range(nseg)]
    b_mixC = [Buf(f"mixC{s}") for s in range(nseg)]
    b_qd = [Buf(f"qd{s}") for s in range(nseg)]
    b_kd = [Buf(f"kd{s}") for s in range(nseg)]
    b_vd = [Buf(f"vd{s}") for s in range(nseg)]

    def sb(name, shape, dt):
        return nc.alloc_sbuf_tensor("sb_" + name, shape, dt)

    ident = sb("ident", [128, 128], BF16)
    b_ident = Buf("ident")
    g.op("sp", lambda e: e.dma_start(out=ident[:], in_=ident_in.ap()), writes=[b_ident], dma=True)
    Gm = sb("Gm", [128, D], F32)
    SHm = sb("SHm", [128, D], F32)
    GTm = sb("GTm", [128, D], F32)
    gf_bc = sb("gf_bc", [128, D], F32)
    b_G, b_SH, b_GT, b_gf = Buf("G"), Buf("SH"), Buf("GT"), Buf("gf")
    g.op("sp", lambda e: e.dma_start(out=gf_bc[:], in_=g_final[0:1, :].partition_broadcast(128)), writes=[b_gf], dma=True)
    NXB = 3
    xt = [sb(f"xt{i}", [128, D], F32) for i in range(NXB)]
    b_xt = [Buf(f"xt{i}") for i in range(NXB)]
    tmpf = [sb(f"tmpf{i}", [128, D], F32) for i in range(2)]
    b_tmpf = [Buf("tmpf0"), Buf("tmpf1")]
    hb = [sb(f"hb{i}", [128, D], BF16) for i in range(2)]
    b_hb = [Buf("hb0"), Buf("hb1")]
    yo1 = sb("yo", [128, D], F32)
    b_yo1 = Buf("yo")
    junk = sb("junk", [128, D], BF16)
    b_junk = Buf("junk")
    ss = [sb(f"ss{i}", [128, 4], F32) for i in range(2)]
    rs = [sb(f"rs{i}", [128, 4], F32) for i in range(2)]
    b_ss = [Buf("ss0"), Buf("ss1")]
    b_rs = [Buf("rs0"), Buf("rs1")]
    fs = [sb(f"fs{i}", [128, 2], F32) for i in range(2)]
    b_fs = [Buf("fs0"), Buf("fs1")]

    ARENA = 159 * 1024
    arena = sb("arena", [128, ARENA // 2], BF16)
    arena_f = arena.bitcast(F32)

    hiwater = {}

    class Carver:
        def __init__(self, tag="?"):
            self.off = 0
            self.tag = tag

        def take(self, shape, dt):
            esz = 2 if dt == BF16 else 4
            n = 1
            for d_ in shape[1:]:
                n *= d_
            self.off = (self.off + 31) // 32 * 32
            assert self.off + n * esz <= ARENA, (self.off, n * esz, ARENA)
            base = arena if dt == BF16 else arena_f
            e0 = self.off // esz
            v = base[:, e0:e0 + n]
            self.off += n * esz
            hiwater[self.tag] = max(hiwater.get(self.tag, 0), self.off)
            if len(shape) == 3:
                v = v.rearrange("p (a b) -> p a b", a=shape[1])
            elif len(shape) == 4:
                v = v.rearrange("p (a b c) -> p a b c", a=shape[1], b=shape[2])
            return v

    banks = [nc.alloc_psum_tensor(f"ps_bank{i}", [128, 512], F32) for i in range(8)]
    banksT = [b.bitcast(BF16) for b in banks]
    b_banks = [Buf(f"bank{i}") for i in range(8)]
    bank_rr = [0]

    def next_bank():
        i = bank_rr[0] % 8
        bank_rr[0] += 1
        return banks[i], banksT[i], b_banks[i]

    cv = Carver()
    c_sb = cv.take([128, nseg, 8], F32)
    c_act = cv.take([128, nseg, 8], F32)
    c_rep = cv.take([128, nseg, 8, 128], BF16)
    wa = [cv.take([128, 8, 512], BF16) for _ in range(2)]
    bb_sb = [cv.take([128, 512], F32) for _ in range(2)]
    mo = [cv.take([128, 512], F32) for _ in range(2)]
    b_c, b_cact, b_crep = Buf("c"), Buf("cact"), Buf("crep")
    b_wa = [Buf("wa0"), Buf("wa1")]
    b_bb = [Buf("bb0"), Buf("bb1")]
    b_mo = [Buf("mo0"), Buf("mo1")]
    b_modbc = Buf("modbc")
    g.op("sp", lambda e: e.dma_start(out=c_sb, in_=c_in.ap().rearrange("s p k -> p s k")), writes=[b_c], dma=True)
    g.op("act", lambda e: e.activation(out=c_act, in_=c_sb, func=AF.Silu), reads=[b_c], writes=[b_cact])
    g.op("dve", lambda e: e.tensor_copy(out=c_rep, in_=c_act.unsqueeze(3).to_broadcast([128, nseg, 8, 128])),
         reads=[b_cact], writes=[b_crep])
    it = 0
    for l in range(DEPTH):
        for cb in range(12):
            k = it % 2
            it += 1
            g.op("pool", lambda e, l=l, cb=cb, k=k: e.dma_start(
                out=wa[k], in_=w_ada[l, :, cb * 512:(cb + 1) * 512].rearrange("(kc p) n -> p kc n", p=128)),
                writes=[b_wa[k]], dma=True)
            g.op("sp", lambda e, l=l, cb=cb, k=k: e.dma_start(
                out=bb_sb[k], in_=b_ada[l:l + 1, cb * 512:(cb + 1) * 512].partition_broadcast(128)),
                writes=[b_bb[k]], dma=True)
            for s in range(nseg):
                bk, _, b_bk = next_bank()
                for kc in range(8):
                    g.op("pe", lambda e, bk=bk, s=s, kc=kc, k=k: e.matmul(
                        bk[:], lhsT=c_rep[:, s, kc, :], rhs=wa[k][:, kc, :], start=(kc == 0), stop=(kc == 7)),
                        reads=[b_crep, b_wa[k]], writes=[b_bk])
                m = (it + s) % 2
                g.op("dve", lambda e, bk=bk, k=k, m=m: e.tensor_tensor(out=mo[m], in0=bk[:], in1=bb_sb[k], op=ALU.add),
                     reads=[b_bk, b_bb[k]], writes=[b_mo[m]])
                g.op("sp", lambda e, s=s, l=l, cb=cb, m=m: e.dma_start(
                    out=modbc[s, l, :, cb * 512:(cb + 1) * 512], in_=mo[m]), reads=[b_mo[m]], writes=[b_modbc], dma=True, multi=True)

    b_xres = [Buf(f"xres{s}") for s in range(nseg)]
    finals = []
    cnt = {"tt": 0}

    def load_mod(s, l, sub, gvec):
        o = 3 * sub * D
        g.op("sp", lambda e: e.dma_start(out=tmpf[1][:], in_=gvec[l:l + 1, :].partition_broadcast(128)), writes=[b_tmpf[1]], dma=True)
        g.op("sp", lambda e: e.dma_start(out=SHm[:], in_=modbc[s, l, :, o:o + D]), reads=[b_modbc], writes=[b_SH], dma=True)
        g.op("sp", lambda e: e.dma_start(out=Gm[:], in_=modbc[s, l, :, o + D:o + 2 * D]), reads=[b_modbc], writes=[b_G], dma=True)
        g.op("sp", lambda e: e.dma_start(out=GTm[:], in_=modbc[s, l, :, o + 2 * D:o + 3 * D]), reads=[b_modbc], writes=[b_GT], dma=True)
        g.op("dve", lambda e: e.scalar_tensor_tensor(out=Gm[:], in0=Gm[:], scalar=1.0, in1=tmpf[1][:], op0=ALU.add, op1=ALU.mult),
             reads=[b_G, b_tmpf[1]], writes=[b_G])

    def norm_a(src_ap_fn, s, tg, tgsize, src_bufs, hbs, b_hbs):
        nt = tgsize // 128
        assert len(hbs) >= nt
        k2 = tg % 2
        g.op("dve", lambda e: e.memset(ss[k2][:], 0.0), writes=[b_ss[k2]])
        for j in range(nt):
            xi = cnt["tt"] % NXB
            cnt["tt"] += 1
            t0 = tg * tgsize + j * 128
            g.op("sp", lambda e, xi=xi, t0=t0: e.dma_start(out=xt[xi][:], in_=src_ap_fn(s, t0)),
                 reads=src_bufs, writes=[b_xt[xi]], dma=True)
            g.op("act", lambda e, xi=xi, j=j: e.activation(out=junk[:], in_=xt[xi][:], func=AF.Square,
                                                          accum_out=ss[k2][:, j:j + 1]),
                 reads=[b_xt[xi]], writes=[b_junk, b_ss[k2]])
            g.op("dve", lambda e, j=j: e.tensor_scalar(out=rs[k2][:, j:j + 1], in0=ss[k2][:, j:j + 1], scalar1=1.0 / D, scalar2=EPS,
                                                       op0=ALU.mult, op1=ALU.add), reads=[b_ss[k2]], writes=[b_rs[k2]])
            g.op("act", lambda e, j=j: e.activation(out=rs[k2][:, j:j + 1], in_=rs[k2][:, j:j + 1], func=AF.Sqrt),
                 reads=[b_rs[k2]], writes=[b_rs[k2]])
            g.op("dve", lambda e, j=j: e.reciprocal(out=rs[k2][:, j:j + 1], in_=rs[k2][:, j:j + 1]), reads=[b_rs[k2]], writes=[b_rs[k2]])
            m = j % 2
            g.op("dve", lambda e, xi=xi, j=j, m=m: e.scalar_tensor_tensor(
                out=tmpf[m][:], in0=xt[xi][:], scalar=rs[k2][:, j:j + 1], in1=Gm[:], op0=ALU.mult, op1=ALU.mult),
                reads=[b_xt[xi], b_rs[k2], b_G], writes=[b_tmpf[m]])
            g.op("pool", lambda e, m=m, j=j: e.tensor_tensor(out=hbs[j], in0=tmpf[m][:], in1=SHm[:], op=ALU.add),
                 reads=[b_tmpf[m], b_SH], writes=[b_hbs[j]])

    def norm_b(tgsize, hTk, b_hTk, hbs, b_hbs):
        for j in range(tgsize // 128):
            bk, bkT, b_bk = next_bank()
            for kc in range(8):
                g.op("pe", lambda e, bkT=bkT, j=j, kc=kc: e.transpose(
                    out=bkT[:, kc * 128:(kc + 1) * 128], in_=hbs[j][:, kc * 128:(kc + 1) * 128], identity=ident[:]),
                    reads=[b_hbs[j], b_ident], writes=[b_bk])
            g.op("act", lambda e, bkT=bkT, j=j: e.copy(
                out=hTk[:, :, j * 128:(j + 1) * 128], in_=bkT[:, 0:1024].rearrange("p (k t) -> p k t", k=8)),
                reads=[b_bk], writes=[b_hTk], multi=(j != 0))

    def phase_inproj(l, s, src_is_input, rope_mode="full"):
        use_rope = rope_mode == "full"
        g.barrier()
        cv = Carver("P1 inproj")
        win = cv.take([128, 8, PROJ], BF16)
        hT = [cv.take([128, 8, 512], BF16) for _ in range(2)]
        hbs = [hb[0][:], hb[1][:], cv.take([128, D], BF16), cv.take([128, D], BF16)]
        b_hbs = [b_hb[0], b_hb[1], Buf("hb2"), Buf("hb3")]
        qb = [cv.take([128, 8, 64], BF16) for _ in range(2)]
        kb = [cv.take([128, 8, 64], BF16) for _ in range(2)]
        vb = [cv.take([128, 512], BF16) for _ in range(2)]
        rp = [cv.take([128, 16], F32) for _ in range(2)]
        rt = [cv.take([128, 8, 8], F32) for _ in range(4)]
        b_win = Buf("win")
        b_hT = [Buf("hT0"), Buf("hT1")]
        b_qb = [Buf("qb0"), Buf("qb1")]
        b_kb = [Buf("kb0"), Buf("kb1")]
        b_vb = [Buf("vb0"), Buf("vb1")]
        b_rp = [Buf("rp0"), Buf("rp1")]
        b_rt = [Buf(f"rt{i}") for i in range(4)]
        g.op("pool", lambda e: e.dma_start(out=win, in_=w_in[l].rearrange("(kc p) n -> p kc n", p=128)), writes=[b_win], dma=True)
        load_mod(s, l, 0, g_mix)
        full = rope_mode == "full"
        if full:
            zst = [cv.take([128, 2, 512], F32) for _ in range(2)]
            zch = [cv.take([128, 2, 528], F32) for _ in range(2)]
            zz8 = cv.take([128, 2, 8], F32)
            T1 = cv.take([128, 2, 528], F32)
            T2 = cv.take([128, 2, 528], F32)
            T3 = cv.take([128, 2, 528], F32)
            Sb = cv.take([128, 2, 512], F32)
            pbf2 = [cv.take([128, 2, 512], BF16) for _ in range(2)]
            yab = [cv.take([128, 2, 512], BF16) for _ in range(2)]
            pwbd = cv.take([128, 2, 128], BF16)
            psc = cv.take([128, 2], F32)
            pic = cv.take([128, 2, 2, 8], F32)
            piw = cv.take([128, 2], F32)
            etmp = cv.take([128, 2, 8], F32)
            b_T1, b_T2, b_T3, b_Sb = Buf("T1"), Buf("T2"), Buf("T3"), Buf("Sb")
            b_pbf2 = [Buf("pbf0"), Buf("pbf1")]
            b_zst = [Buf("zst0"), Buf("zst1")]
            b_zch = [Buf("zch0"), Buf("zch1")]
            b_zz8 = Buf("zz8")
            b_yab = [Buf("yab0"), Buf("yab1")]
            b_pw, b_psc, b_pic, b_piw, b_etmp = Buf("pwbd"), Buf("psc"), Buf("pic"), Buf("piw"), Buf("etmp")
            g.op("pool", lambda e: e.memset(pwbd, 0.0), writes=[b_pw])
            for gi in range(4):
                g.op("pool", lambda e, gi=gi: e.dma_start(out=pwbd[(gi % 2) * 64:(gi % 2) * 64 + 64, gi // 2, (gi % 2) * 64:(gi % 2) * 64 + 64],
                                                        in_=pool_w[l, gi]), writes=[b_pw], dma=True, multi=True)
            g.op("sp", lambda e: e.dma_start(out=psc, in_=pool_sc[l]), writes=[b_psc], dma=True)
            g.op("sp", lambda e: e.dma_start(out=pic, in_=pool_ic.ap()), writes=[b_pic], dma=True)
            g.op("sp", lambda e: e.dma_start(out=piw, in_=pool_iw.ap()), writes=[b_piw], dma=True)
            g.op("pool", lambda e: e.memset(zz8, 0.0), writes=[b_zz8])
            g.op("sp", lambda e: e.dma_start(out=zaD[s, :, :, 0:8], in_=zz8), reads=[b_zz8], writes=[b_zaD[s]], dma=True)
            g.op("sp", lambda e: e.dma_start(out=zaD[s, :, :, S + 8:S + 16], in_=zz8), reads=[b_zz8], writes=[b_zaD[s]], dma=True, multi=True)
            swn = cv.take([128, 4, 128], BF16)
            swT = cv.take([128, 4, 128], BF16)
            sbias = cv.take([128, 2, 128], F32)
            vg = [cv.take([128, 4, 64], F32) for _ in range(2)]
            vc = [cv.take([128, 4, 64], F32) for _ in range(2)]
            vsq = cv.take([128, 4, 64], F32)
            st = [cv.take([128, 8], F32) for _ in range(2)]
            vnpad = [cv.take([128, 4, 128], BF16) for _ in range(2)]
            uT = [cv.take([128, 2, 512], F32) for _ in range(2)]
            yct = cv.take([128, 2, 128], F32)
            ycb = [cv.take([128, 2, 512], BF16) for _ in range(2)]
            b_swn, b_swT, b_sbias, b_vsq, b_yct = Buf("swn"), Buf("swT"), Buf("sbias"), Buf("vsq"), Buf("yct")
            b_vg = [Buf("vg0"), Buf("vg1")]
            b_vc = [Buf("vc0"), Buf("vc1")]
            b_st = [Buf("st0"), Buf("st1")]
            b_vn = [Buf("vn0"), Buf("vn1")]
            b_uT = [Buf("uT0"), Buf("uT1")]
            b_ycb = [Buf("ycb0"), Buf("ycb1")]
            g.op("pool", lambda e: e.dma_start(out=swn, in_=sgu_w[l].rearrange("g p q -> p g q")), writes=[b_swn], dma=True)
            bk, bkT, b_bk = next_bank()
            for gi in range(4):
                g.op("pe", lambda e, bkT=bkT, gi=gi: e.transpose(out=bkT[:, gi * 128:(gi + 1) * 128], in_=swn[:, gi, :], identity=ident[:]),
                     reads=[b_swn, b_ident], writes=[b_bk])
            g.op("act", lambda e, bkT=bkT: e.copy(out=swT, in_=bkT[:, 0:512].rearrange("p (g q) -> p g q", g=4)), reads=[b_bk], writes=[b_swT])
            for gi in range(4):
                g.op("sp", lambda e, gi=gi: e.dma_start(out=sbias[(gi % 2) * 64:(gi % 2) * 64 + 64, gi // 2, :],
                                                      in_=sgu_b[l, gi:gi + 1, :].partition_broadcast(64)), writes=[b_sbias], dma=True, multi=(gi != 0))
            for i in range(2):
                g.op("pool", lambda e, i=i: e.memset(vnpad[i], 0.0), writes=[b_vn[i]])

        def src_ap(s_, t0):
            return (x_in if src_is_input else xres)[s_, t0:t0 + 128, :]

        src_bufs = [] if src_is_input else [b_xres[s]]
        stores = []

        def rope_evac(bk, b_bk, dst, b_dst, rpk, b_rpk):
            bv = bk[:].rearrange("p (h d) -> p h d", h=8)
            cosb = rpk[:, 0:8].unsqueeze(1).to_broadcast([128, 8, 8])
            sinb = rpk[:, 8:16].unsqueeze(1).to_broadcast([128, 8, 8])
            x1, x2 = bv[:, :, 0:8], bv[:, :, 8:16]
            g.op("dve", lambda e: e.tensor_tensor(out=rt[0], in0=x1, in1=cosb, op=ALU.mult), reads=[b_bk, b_rpk], writes=[b_rt[0]])
            g.op("dve", lambda e: e.tensor_tensor(out=rt[1], in0=x2, in1=sinb, op=ALU.mult), reads=[b_bk, b_rpk], writes=[b_rt[1]])
            g.op("dve", lambda e: e.tensor_tensor(out=rt[2], in0=x2, in1=cosb, op=ALU.mult), reads=[b_bk, b_rpk], writes=[b_rt[2]])
            g.op("dve", lambda e: e.tensor_tensor(out=rt[3], in0=x1, in1=sinb, op=ALU.mult), reads=[b_bk, b_rpk], writes=[b_rt[3]])
            g.op("dve", lambda e: e.tensor_copy(out=dst[:, :, 16:64], in_=bv[:, :, 16:64]), reads=[b_bk], writes=[b_dst])
            g.op("pool", lambda e: e.tensor_tensor(out=dst[:, :, 0:8], in0=rt[0], in1=rt[1], op=ALU.subtract),
                 reads=[b_rt[0], b_rt[1]], writes=[b_dst])
            g.op("pool", lambda e: e.tensor_tensor(out=dst[:, :, 8:16], in0=rt[2], in1=rt[3], op=ALU.add),
                 reads=[b_rt[2], b_rt[3]], writes=[b_dst])

        def pool_elem(tc):
            NCH = S // 512
            pbf, b_pbf = pbf2[tc % 2], b_pbf2[tc % 2]
            if True:
                kz = tc % 2
                g.op("sp", lambda e, kz=kz, tc=tc: e.dma_start(out=zch[kz], in_=zaD[s, :, :, tc * 512:tc * 512 + 528]),
                     reads=[b_zaD[s]], writes=[b_zch[kz]], dma=True)
                b_zaT = b_zch[kz]
                Z = lambda off, n, kz=kz: zch[kz][:, :, 8 + off:8 + off + n]
                g.op("pool", lambda e, Z=Z: e.tensor_tensor(out=T1[:, :, 1:528], in0=Z(-7, 527), in1=Z(-8, 527), op=ALU.add), reads=[b_zaT], writes=[b_T1])
                g.op("pool", lambda e: e.tensor_tensor(out=T2[:, :, 3:528], in0=T1[:, :, 3:528], in1=T1[:, :, 1:526], op=ALU.add), reads=[b_T1], writes=[b_T2])
                g.op("pool", lambda e: e.tensor_tensor(out=T3[:, :, 7:528], in0=T2[:, :, 7:528], in1=T2[:, :, 3:524], op=ALU.add), reads=[b_T2], writes=[b_T3])
                g.op("pool", lambda e: e.tensor_copy(out=Sb[0:64, 0, :], in_=T1[0:64, 0, 8:520]), reads=[b_T1], writes=[b_Sb])
                g.op("pool", lambda e: e.tensor_copy(out=Sb[64:128, 0, :], in_=T2[64:128, 0, 9:521]), reads=[b_T2], writes=[b_Sb], multi=True)
                g.op("pool", lambda e: e.tensor_copy(out=Sb[0:64, 1, :], in_=T3[0:64, 1, 11:523]), reads=[b_T3], writes=[b_Sb], multi=True)
                g.op("pool", lambda e: e.tensor_tensor(out=Sb[64:128, 1, :], in0=T3[64:128, 1, 15:527], in1=T3[64:128, 1, 7:519], op=ALU.add),
                     reads=[b_T3], writes=[b_Sb], multi=True)
                for cc in range(2):
                    g.op("pool", lambda e, cc=cc: e.tensor_tensor(out=T1[:, cc, 0:512], in0=Sb[:, cc, :], in1=piw[:, cc:cc + 1].to_broadcast([128, 512]), op=ALU.mult),
                         reads=[b_Sb, b_piw], writes=[b_T1], multi=(cc != 0))
                    g.op("pool", lambda e, cc=cc, Z=Z: e.tensor_tensor(out=pbf[:, cc, :], in0=T1[:, cc, 0:512], in1=Z(0, 512)[:, cc, :], op=ALU.subtract),
                         reads=[b_T1, b_zaT], writes=[b_pbf], multi=(cc != 0))
                for edge, cols in ((0, slice(0, 8)), (1, slice(504, 512))):
                    if (edge == 0 and tc == 0) or (edge == 1 and tc == NCH - 1):
                        off = 0 if edge == 0 else 504
                        g.op("pool", lambda e, edge=edge, cols=cols: e.tensor_tensor(out=etmp, in0=Sb[:, :, cols], in1=pic[:, :, edge, :], op=ALU.mult),
                             reads=[b_Sb, b_pic], writes=[b_etmp])
                        g.op("pool", lambda e, cols=cols, Z=Z, off=off: e.tensor_tensor(out=pbf[:, :, cols], in0=etmp, in1=Z(off, 8), op=ALU.subtract),
                             reads=[b_etmp, b_zaT, b_pbf], writes=[b_pbf])
        def pool_mm(tc):
            pbf, b_pbf = pbf2[tc % 2], b_pbf2[tc % 2]
            if True:
                ky = tc % 2
                for cc in range(2):
                    bk, _, b_bk = next_bank()
                    g.op("pe", lambda e, bk=bk, cc=cc: e.matmul(bk[:], lhsT=pwbd[:, cc, :], rhs=pbf[:, cc, :], start=True, stop=True),
                         reads=[b_pw, b_pbf], writes=[b_bk])
                    g.op("act", lambda e, bk=bk, cc=cc, ky=ky: e.activation(out=yab[ky][:, cc, :], in_=bk[:], func=AF.Copy, scale=psc[:, cc:cc + 1]),
                         reads=[b_bk, b_psc], writes=[b_yab[ky]], multi=(cc != 0))
                stores.append(g.op("sp", lambda e, ky=ky, tc=tc: e.dma_start(
                    out=mixA[s, :, tc * 512:(tc + 1) * 512].rearrange("(c p) t -> p c t", p=128), in_=yab[ky]),
                    reads=[b_yab[ky]], writes=[b_mixA[s]], dma=True, multi=True))

        def sgu_part2(m, k, j, tg):
            cbk, _, b_cbk = next_bank()
            for pr in range(2):
                for gh in range(2):
                    gi = 2 * pr + gh
                    g.op("pe", lambda e, cbk=cbk, pr=pr, gh=gh, gi=gi, m=m: e.matmul(
                        cbk[:, pr * 128:(pr + 1) * 128], lhsT=vnpad[m][:, gi, :], rhs=swT[:, gi, :], start=(gh == 0), stop=(gh == 1)),
                        reads=[b_vn[m], b_swT], writes=[b_cbk])
            g.op("dve", lambda e, cbk=cbk: e.tensor_tensor(out=yct, in0=cbk[:, 0:256].rearrange("p (a q) -> p a q", a=2), in1=sbias, op=ALU.add),
                 reads=[b_cbk, b_sbias], writes=[b_yct])
            g.op("dve", lambda e, k=k, j=j: e.tensor_tensor(out=ycb[k][:, :, j * 128:(j + 1) * 128], in0=yct, in1=uT[k][:, :, j * 128:(j + 1) * 128], op=ALU.mult),
                 reads=[b_yct, b_uT[k]], writes=[b_ycb[k]], multi=(j != 0))
            if j == 3:
                stores.append(g.op("sp", lambda e, k=k, tg=tg: e.dma_start(
                    out=mixC[s, :, tg * 512:(tg + 1) * 512].rearrange("(c p) t -> p c t", p=128), in_=ycb[k]),
                    reads=[b_ycb[k]], writes=[b_mixC[s]], dma=True, multi=True))

        sgu_pend = [None]
        NTG1 = S // 512
        norm_a(src_ap, s, 0, 512, src_bufs, hbs, b_hbs)
        norm_b(512, hT[0], b_hT[0], hbs, b_hbs)
        for tg in range(NTG1):
            k = tg % 2
            if full:
                for cc in range(2):
                    bk, _, b_bk = next_bank()
                    for kc in range(8):
                        g.op("pe", lambda e, bk=bk, kc=kc, cc=cc, k=k: e.matmul(
                            bk[:], lhsT=win[:, kc, cc * 128:(cc + 1) * 128], rhs=hT[k][:, kc, :], start=(kc == 0), stop=(kc == 7)),
                            reads=[b_win, b_hT[k]], writes=[b_bk])
                    g.op("dve", lambda e, bk=bk, cc=cc, k=k: e.tensor_copy(out=zst[k][:, cc, :], in_=bk[:]),
                         reads=[b_bk], writes=[b_zst[k]], multi=(cc != 0))
                g.op("sp", lambda e, k=k, tg=tg: e.dma_start(out=zaD[s, :, :, 8 + tg * 512:8 + (tg + 1) * 512], in_=zst[k]),
                     reads=[b_zst[k]], writes=[b_zaD[s]], dma=True, multi=True)
                for cc in range(2):
                    bk, _, b_bk = next_bank()
                    for kc in range(8):
                        g.op("pe", lambda e, bk=bk, kc=kc, cc=cc, k=k: e.matmul(
                            bk[:], lhsT=win[:, kc, ZCU + cc * 128:ZCU + (cc + 1) * 128], rhs=hT[k][:, kc, :], start=(kc == 0), stop=(kc == 7)),
                            reads=[b_win, b_hT[k]], writes=[b_bk])
                    g.op("act", lambda e, bk=bk, cc=cc, k=k: e.activation(out=uT[k][:, cc, :], in_=bk[:], func=AF.Gelu_apprx_tanh),
                         reads=[b_bk], writes=[b_uT[k]], multi=(cc != 0))
            if tg + 1 < NTG1:
                norm_a(src_ap, s, tg + 1, 512, src_bufs, hbs, b_hbs)
            for j in range(4):
                t0 = tg * 512 + j * 128
                m = j % 2
                if full:
                    bk, _, b_bk = next_bank()
                    for kc in range(8):
                        g.op("pe", lambda e, bk=bk, kc=kc, j=j, k=k: e.matmul(
                            bk[:, 0:256], lhsT=hT[k][:, kc, j * 128:(j + 1) * 128], rhs=win[:, kc, ZCV:ZCV + 256],
                            start=(kc == 0), stop=(kc == 7)), reads=[b_hT[k], b_win], writes=[b_bk])
                    g.op("act", lambda e, bk=bk, m=m: e.activation(out=vg[m].rearrange("p g c -> p (g c)"), in_=bk[:, 0:256], func=AF.Gelu_apprx_tanh),
                         reads=[b_bk], writes=[b_vg[m]])
                    g.op("dve", lambda e, m=m: e.tensor_reduce(out=st[m][:, 0:4], in_=vg[m], axis=mybir.AxisListType.X, op=ALU.add),
                         reads=[b_vg[m]], writes=[b_st[m]])
                    g.op("dve", lambda e, m=m: e.tensor_scalar(out=st[m][:, 0:4], in0=st[m][:, 0:4], scalar1=1.0 / 64, scalar2=None, op0=ALU.mult),
                         reads=[b_st[m]], writes=[b_st[m]])
                    g.op("dve", lambda e, m=m: e.tensor_tensor(out=vc[m], in0=vg[m], in1=st[m][:, 0:4].unsqueeze(2).to_broadcast([128, 4, 64]), op=ALU.subtract),
                         reads=[b_vg[m], b_st[m]], writes=[b_vc[m]])
                    g.op("dve", lambda e, m=m: e.tensor_tensor(out=vsq, in0=vc[m], in1=vc[m], op=ALU.mult), reads=[b_vc[m]], writes=[b_vsq])
                    g.op("dve", lambda e, m=m: e.tensor_reduce(out=st[m][:, 4:8], in_=vsq, axis=mybir.AxisListType.X, op=ALU.add),
                         reads=[b_vsq], writes=[b_st[m]])
                    g.op("dve", lambda e, m=m: e.tensor_scalar(out=st[m][:, 4:8], in0=st[m][:, 4:8], scalar1=1.0 / 64, scalar2=EPS, op0=ALU.mult, op1=ALU.add),
                         reads=[b_st[m]], writes=[b_st[m]])
                    g.op("act", lambda e, m=m: e.activation(out=st[m][:, 4:8], in_=st[m][:, 4:8], func=AF.Sqrt), reads=[b_st[m]], writes=[b_st[m]])
                    g.op("dve", lambda e, m=m: e.reciprocal(out=st[m][:, 4:8], in_=st[m][:, 4:8]), reads=[b_st[m]], writes=[b_st[m]])
                    for gi in range(4):
                        g.op("dve", lambda e, m=m, gi=gi: e.tensor_scalar(
                            out=vnpad[m][:, gi, (gi % 2) * 64:(gi % 2) * 64 + 64], in0=vc[m][:, gi, :], scalar1=st[m][:, 4 + gi:5 + gi], scalar2=None, op0=ALU.mult),
                            reads=[b_vc[m], b_st[m]], writes=[b_vn[m]], multi=(gi != 0))
                if use_rope or rope_mode == "loads":
                    g.op("sp", lambda e, t0=t0, m=m: e.dma_start(out=rp[m], in_=rope_in[t0:t0 + 128, :]), writes=[b_rp[m]], dma=True)
                for name, col0 in (("q", QO), ("k", KO), ("v", VO)):
                    bk, _, b_bk = next_bank()
                    for kc in range(8):
                        g.op("pe", lambda e, bk=bk, kc=kc, j=j, k=k, col0=col0: e.matmul(
                            bk[:], lhsT=hT[k][:, kc, j * 128:(j + 1) * 128], rhs=win[:, kc, col0:col0 + 512],
                            start=(kc == 0), stop=(kc == 7)), reads=[b_hT[k], b_win], writes=[b_bk])
                    if not use_rope and name in ("q", "k"):
                        dst_, b_dst_ = (qb[m], b_qb[m]) if name == "q" else (kb[m], b_kb[m])
                        g.op("act", lambda e, bk=bk, dst_=dst_: e.copy(out=dst_.rearrange("p h d -> p (h d)"), in_=bk[:]),
                             reads=[b_bk], writes=[b_dst_])
                    elif name == "q":
                        rope_evac(bk, b_bk, qb[m], b_qb[m], rp[m], b_rp[m])
                    elif name == "k":
                        rope_evac(bk, b_bk, kb[m], b_kb[m], rp[m], b_rp[m])
                    if name == "q":
                        stores.append(g.op("sp", lambda e, t0=t0, m=m: e.dma_start(
                            out=qd[s, t0:t0 + 128, :], in_=qb[m].rearrange("p h d -> p (h d)")),
                            reads=[b_qb[m]], writes=[b_qd[s]], dma=True, multi=True))
                    elif name == "k":
                        stores.append(g.op("sp", lambda e, t0=t0, m=m: e.dma_start(
                            out=kd[s, t0:t0 + 128, :], in_=kb[m].rearrange("p h d -> p (h d)")),
                            reads=[b_kb[m]], writes=[b_kd[s]], dma=True, multi=True))
                    else:
                        g.op("act", lambda e, bk=bk, m=m: e.copy(out=vb[m], in_=bk[:]), reads=[b_bk], writes=[b_vb[m]])
                        stores.append(g.op("sp", lambda e, t0=t0, m=m: e.dma_start(out=vd[s, t0:t0 + 128, :], in_=vb[m]),
                                           reads=[b_vb[m]], writes=[b_vd[s]], dma=True, multi=True))
                if full:
                    if sgu_pend[0] is not None:
                        sgu_part2(*sgu_pend[0])
                    sgu_pend[0] = (m, k, j, tg)
                if full and tg >= 1 and j == 1:
                    pool_elem(tg - 1)
            if full and tg >= 2:
                pool_mm(tg - 2)
            if tg + 1 < NTG1:
                norm_b(512, hT[1 - k], b_hT[1 - k], hbs, b_hbs)
        if full and sgu_pend[0] is not None:
            sgu_part2(*sgu_pend[0])
        if full:
            pool_elem(S // 512 - 1)
            if S // 512 >= 2:
                pool_mm(S // 512 - 2)
            pool_mm(S // 512 - 1)
        return stores


    PATTERNS = (1, 4, 16)

    def phase_attn(l, s, dump=False, outproj=False, dbg_l0=False):
        g.barrier()
        assert S % 2048 == 0
        cv = Carver("P2 attn")
        kTw = cv.take([128, 4096], BF16)
        qTb = cv.take([128, 2048], BF16)
        NVT = 17 + 20 + 32
        Vt = cv.take([128, NVT, 128], BF16)
        acc = [cv.take([128, 2, 2048], F32) for _ in range(2)]
        pT = [cv.take([128, 512], BF16) for _ in range(4)]
        msk = cv.take([128, 4, 256], BF16)
        ones = cv.take([128, 64], BF16)
        ybT = cv.take([128, 8, 2048], BF16)
        b_kTw, b_qTb, b_msk, b_ones, b_ybT = Buf("kTw"), Buf("qTb"), Buf("msk"), Buf("ones"), Buf("ybT")
        b_V = {d: Buf(f"V{d}") for d in PATTERNS}
        b_acc = [Buf("acc0"), Buf("acc1")]
        b_pT = [Buf(f"pT{i}") for i in range(4)]
        vbase = {1: 0, 4: 17, 16: 37}
        if outproj:
            woA = cv.take([128, 2, D], BF16)
            woC = cv.take([128, 2, D], BF16)
            woB = cv.take([128, 8, D], BF16)
            yaT = cv.take([128, 2, 2048], BF16)
            ycT = cv.take([128, 2, 2048], BF16)
            b_woA, b_woC, b_woB, b_yaT, b_ycT = Buf("woA"), Buf("woC"), Buf("woB"), Buf("yaT"), Buf("ycT")
            g.op("pool", lambda e: e.dma_start(out=woA, in_=w_out[l, 0:256, :].rearrange("(c p) n -> p c n", p=128)), writes=[b_woA], dma=True)
            g.op("pool", lambda e: e.dma_start(out=woC, in_=w_out[l, 768:1024, :].rearrange("(c p) n -> p c n", p=128)), writes=[b_woC], dma=True)
            g.op("pool", lambda e: e.memset(woB, 0.0), writes=[b_woB])
            g.op("pool", lambda e: e.dma_start(out=woB[0:64, :, :], in_=w_out[l, 256:768, :].rearrange("(h p) n -> p h n", p=64)),
                 writes=[b_woB], dma=True, multi=True)
            load_mod(s, l, 0, g_mix)
        g.op("sp", lambda e: e.dma_start(out=msk, in_=masks_in.ap()), writes=[b_msk], dma=True)
        g.op("sp", lambda e: e.dma_start(out=ones, in_=ones_in.ap()), writes=[b_ones], dma=True)
        g.op("pool", lambda e: e.memset(ybT, 0.0), writes=[b_ybT])
        g.op("pool", lambda e: e.memset(kTw, 0.0), writes=[b_kTw])
        g.op("pool", lambda e: e.memset(Vt, 0.0), writes=[b_V[1], b_V[4], b_V[16]])
        NSB = S // 2048
        pcount = [0]
        outs = []
        for SB in range(NSB):
            T0 = SB * 2048
            for hp in range(4):
                cs = slice(hp * 128, (hp + 1) * 128)
                w_lo, w_hi = max(T0 - 1024, 0), min(T0 + 3072, S)
                c_lo = w_lo - (T0 - 1024)
                for t in range(w_lo, w_hi, 512):
                    g.op("sp", lambda e, t=t, c0=c_lo + (t - w_lo), cs=cs: e.dma_start_transpose(
                        out=kTw[:, c0:c0 + 512], in_=kd[s, t:t + 512, cs]), reads=[b_kd[s]], writes=[b_kTw], dma=True, multi=(t != w_lo))
                for t in range(0, 2048, 512):
                    g.op("sp", lambda e, t=t, cs=cs, T0=T0: e.dma_start_transpose(
                        out=qTb[:, t:t + 512], in_=qd[s, T0 + t:T0 + t + 512, cs]), reads=[b_qd[s]], writes=[b_qTb], dma=True, multi=(t != 0))
                if w_lo > T0 - 1024:
                    g.op("pool", lambda e, c_lo=c_lo: e.memset(kTw[:, 0:c_lo], 0.0), writes=[b_kTw], multi=True)
                if w_hi < T0 + 3072:
                    g.op("pool", lambda e, c1=w_hi - (T0 - 1024): e.memset(kTw[:, c1:4096], 0.0), writes=[b_kTw], multi=True)
                for d in PATTERNS:
                    nb = 16 // d
                    Lf = S // d
                    for r in range(d):
                        for u in range(nb + 1):
                            kf0 = (SB * nb + u) * 128 - 64
                            p0, p1 = max(0, -kf0), min(128, Lf - kf0)
                            slot = vbase[d] + r * (nb + 1) + u
                            row0 = (kf0 + p0) * d + r
                            rows = slice(row0, row0 + (p1 - p0 - 1) * d + 1, d)
                            g.op("sp", lambda e, slot=slot, p0=p0, p1=p1, rows=rows, cs=cs: e.dma_start(
                                out=Vt[p0:p1, slot, :], in_=vd[s, rows, cs]), reads=[b_vd[s]], writes=[b_V[d]], dma=True,
                                multi=not (r == 0 and u == 0))
                            if p0 > 0:
                                g.op("pool", lambda e, slot=slot, p0=p0: e.memset(Vt[0:p0, slot, :], 0.0), writes=[b_V[d]], multi=True)
                            if p1 < 128:
                                g.op("pool", lambda e, slot=slot, p1=p1: e.memset(Vt[p1:128, slot, :], 0.0), writes=[b_V[d]], multi=True)
                LAG = 3
                items = []
                for pi, d in enumerate(PATTERNS):
                    nb = 16 // d
                    blocks = [(r, j) for r in range(d) for j in range(nb)]
                    for it_ in range(0, len(blocks), 2):
                        for h in range(2):
                            items.append((pi, d, blocks[it_:it_ + 2], h))

                def stage_a(item):
                    pi, d, pair, h = item
                    nb = 16 // d
                    NBf = S // (128 * d)
                    rows = slice(h * 64, (h + 1) * 64)
                    sbk, _, b_sbk = next_bank()
                    for bi, (r, j) in enumerate(pair):
                        q0 = j * 128 * d + r
                        qs = slice(q0, q0 + 127 * d + 1, d)
                        for m in range(2):
                            k0 = (j * 128 - 64 + m * 128) * d + r + 1024
                            ks = slice(k0, k0 + 127 * d + 1, d)
                            c0 = (bi * 2 + m) * 128
                            g.op("pe", lambda e, sbk=sbk, c0=c0, rows=rows, ks=ks, qs=qs: e.matmul(
                                sbk[:, c0:c0 + 128], lhsT=kTw[rows, ks], rhs=qTb[rows, qs], start=True, stop=True),
                                reads=[b_kTw, b_qTb], writes=[b_sbk])
                    pk = pcount[0] % 4
                    pcount[0] += 1
                    g.op("act", lambda e, sbk=sbk, pk=pk: e.activation(out=pT[pk], in_=sbk[:], func=AF.Exp, scale=0.125),
                         reads=[b_sbk], writes=[b_pT[pk]])
                    for bi, (r, j) in enumerate(pair):
                        ib = SB * nb + j
                        mi = (1 if ib == 0 else 0) + (2 if ib == NBf - 1 else 0)
                        eng = "pool"
                        g.op(eng, lambda e, pk=pk, bi=bi, mi=mi: e.tensor_tensor(
                            out=pT[pk][:, bi * 256:(bi + 1) * 256], in0=pT[pk][:, bi * 256:(bi + 1) * 256], in1=msk[:, mi, :], op=ALU.mult),
                            reads=[b_pT[pk], b_msk], writes=[b_pT[pk]])
                    return pk

                def stage_b(item, pk):
                    pi, d, pair, h = item
                    nb = 16 // d
                    rows = slice(h * 64, (h + 1) * 64)
                    obk, _, b_obk = next_bank()
                    for bi, (r, j) in enumerate(pair):
                        for m in range(2):
                            slot = vbase[d] + r * (nb + 1) + j + m
                            pc = (bi * 2 + m) * 128
                            g.op("pe", lambda e, obk=obk, bi=bi, m=m, slot=slot, pk=pk, pc=pc, rows=rows: e.matmul(
                                obk[0:64, bi * 128:(bi + 1) * 128], lhsT=Vt[:, slot, rows], rhs=pT[pk][:, pc:pc + 128],
                                start=(m == 0), stop=(m == 1)), reads=[b_V[d], b_pT[pk]], writes=[b_obk])
                        for m in range(2):
                            pc = (bi * 2 + m) * 128
                            g.op("pe", lambda e, obk=obk, bi=bi, m=m, pk=pk, pc=pc: e.matmul(
                                obk[0:64, 256 + bi * 128:256 + (bi + 1) * 128], lhsT=ones, rhs=pT[pk][:, pc:pc + 128],
                                start=(m == 0), stop=(m == 1)), reads=[b_ones, b_pT[pk]], writes=[b_obk])
                    (r0, j0), (r1, j1) = pair
                    if r0 == r1:
                        a0 = j0 * 128 * d + r0
                        pieces = [(acc[h][0:64, :, a0:a0 + 255 * d + 1:d], obk[0:64, :].rearrange("p (n q) -> p n q", n=2))]
                    else:
                        pieces = []
                        for bi, (r, j) in enumerate(pair):
                            ab = j * 128 * d + r
                            pieces.append((acc[h][0:64, :, ab:ab + 127 * d + 1:d],
                                           obk[0:64, :].rearrange("p (n b q) -> p n b q", n=2, b=2)[:, :, bi, :]))
                    for dst, src in pieces:
                        if pi == 0:
                            g.op("act", lambda e, dst=dst, src=src: e.copy(out=dst, in_=src), reads=[b_obk], writes=[b_acc[h]])
                        else:
                            g.op("dve", lambda e, dst=dst, src=src: e.tensor_tensor(out=dst, in0=dst, in1=src, op=ALU.add),
                                 reads=[b_obk, b_acc[h]], writes=[b_acc[h]])

                pend = []
                for n in range(len(items) + LAG):
                    if n < len(items):
                        pend.append((items[n], stage_a(items[n])))
                    if n >= LAG:
                        stage_b(*pend[n - LAG])
                for h in range(2):
                    hh = hp * 2 + h
                    g.op("dve", lambda e, h=h: e.reciprocal(out=acc[h][0:64, 1, :], in_=acc[h][0:64, 1, :]), reads=[b_acc[h]], writes=[b_acc[h]])
                    g.op("dve", lambda e, h=h, hh=hh: e.tensor_tensor(out=ybT[0:64, hh, :], in0=acc[h][0:64, 0, :], in1=acc[h][0:64, 1, :], op=ALU.mult),
                         reads=[b_acc[h]], writes=[b_ybT], multi=True)
                    if dump:
                        outs.append(g.op("sp", lambda e, hh=hh, T0=T0: e.dma_start(out=dyb[s, hh, :, T0:T0 + 2048], in_=ybT[0:64, hh, :]),
                                         reads=[b_ybT], dma=True))
            if outproj:
                g.op("sp", lambda e, T0=T0: e.dma_start(out=yaT, in_=mixA[s, :, T0:T0 + 2048].rearrange("(c p) t -> p c t", p=128)),
                     reads=[b_mixA[s]], writes=[b_yaT], dma=True)
                g.op("sp", lambda e, T0=T0: e.dma_start(out=ycT, in_=mixC[s, :, T0:T0 + 2048].rearrange("(c p) t -> p c t", p=128)),
                     reads=[b_mixC[s]], writes=[b_ycT], dma=True)
                xsrc = x_in if l == 0 else xres
                for tt in range(16):
                    t0 = T0 + tt * 128
                    tsl = slice(tt * 128, (tt + 1) * 128)
                    xi = cnt["tt"] % NXB
                    cnt["tt"] += 1
                    g.op("sp", lambda e, xi=xi, t0=t0, xsrc=xsrc: e.dma_start(out=xt[xi][:], in_=xsrc[s, t0:t0 + 128, :]),
                         reads=([] if l == 0 else [b_xres[s]]), writes=[b_xt[xi]], dma=True)
                    for half in range(2):
                        hs = slice(half * 512, (half + 1) * 512)
                        bk, _, b_bk = next_bank()
                        ops_ = [(yaT[:, c, tsl], woA[:, c, hs], b_yaT, b_woA) for c in range(2)] + \
                               [(ybT[:, h_, tsl], woB[:, h_, hs], b_ybT, b_woB) for h_ in range(8)] + \
                               [(ycT[:, c, tsl], woC[:, c, hs], b_ycT, b_woC) for c in range(2)]
                        for i_, (lt, rh, bl, br) in enumerate(ops_):
                            g.op("pe", lambda e, bk=bk, lt=lt, rh=rh, i_=i_: e.matmul(bk[:], lhsT=lt, rhs=rh, start=(i_ == 0), stop=(i_ == 11)),
                                 reads=[bl, br], writes=[b_bk])
                        g.op("dve", lambda e, bk=bk, hs=hs: e.tensor_tensor(out=yo1[:, hs], in0=bk[:], in1=GTm[:, hs], op=ALU.mult),
                             reads=[b_bk, b_GT], writes=[b_yo1])
                    g.op("pool", lambda e, xi=xi: e.tensor_tensor(out=yo1[:], in0=yo1[:], in1=xt[xi][:], op=ALU.add),
                         reads=[b_yo1, b_xt[xi]], writes=[b_yo1])
                    st_ = g.op("sp", lambda e, t0=t0: e.dma_start(out=xres[s, t0:t0 + 128, :], in_=yo1[:]),
                               reads=[b_yo1], writes=[b_xres[s]], dma=True, multi=True)
                    if dbg_l0:
                        outs.append(g.op("sp", lambda e, t0=t0: e.dma_start(out=x1dbg[s, t0:t0 + 128, :], in_=yo1[:]), reads=[b_yo1], dma=True))
        return outs

    def phase_mlp(l, s, src_is_input, last):
        g.barrier()
        TG = 256
        NT = TG // 128
        cv = Carver("P3 mlp")
        wup = cv.take([128, 8, DFF], BF16)
        wdn = cv.take([128, 32, D], BF16)
        uT = cv.take([128, 32, TG], BF16)
        hT = [cv.take([128, 8, TG], BF16) for _ in range(2)]
        rl = [cv.take([128, TG], F32) for _ in range(4)]
        b_wup, b_wdn = Buf("wup"), Buf("wdn")
        b_uT = [Buf(f"uT{i}") for i in range(32)]
        b_hT = [Buf("hT0"), Buf("hT1")]
        b_rl = [Buf(f"rl{i}") for i in range(4)]
        g.op("pool", lambda e: e.dma_start(out=wup, in_=w_up[l].rearrange("(kc p) n -> p kc n", p=128)), writes=[b_wup], dma=True)
        g.op("pool", lambda e: e.dma_start(out=wdn, in_=w_down[l].rearrange("(fc p) n -> p fc n", p=128)), writes=[b_wdn], dma=True)
        load_mod(s, l, 1, g_mlp)

        def src_ap(s_, t0):
            return (x_in if src_is_input else xres)[s_, t0:t0 + 128, :]

        src_bufs = [] if src_is_input else [b_xres[s]]
        hbs3 = [hb[0][:], hb[1][:]]
        NTG3 = S // TG
        norm_a(src_ap, s, 0, TG, src_bufs, hbs3, b_hb)
        norm_b(TG, hT[0], b_hT[0], hbs3, b_hb)
        for tg in range(NTG3):
            k = tg % 2
            for fc in range(32):
                bk, _, b_bk = next_bank()
                for kc in range(8):
                    g.op("pe", lambda e, bk=bk, kc=kc, fc=fc, k=k: e.matmul(
                        bk[:, 0:TG], lhsT=wup[:, kc, fc * 128:(fc + 1) * 128], rhs=hT[k][:, kc, :], start=(kc == 0), stop=(kc == 7)),
                        reads=[b_wup, b_hT[k]], writes=[b_bk])
                m = fc % 4
                if fc % 2 == 0:
                    g.op("act", lambda e, bk=bk, m=m: e.activation(out=rl[m], in_=bk[:, 0:TG], func=AF.Relu),
                         reads=[b_bk], writes=[b_rl[m]])
                    g.op("pool", lambda e, fc=fc, m=m: e.tensor_tensor(out=uT[:, fc, :], in0=rl[m], in1=rl[m], op=ALU.mult),
                         reads=[b_rl[m]], writes=[b_uT[fc]])
                else:
                    g.op("dve", lambda e, bk=bk, m=m: e.tensor_scalar(out=rl[m], in0=bk[:, 0:TG], scalar1=0.0, scalar2=None, op0=ALU.max),
                         reads=[b_bk], writes=[b_rl[m]])
                    g.op("dve", lambda e, fc=fc, m=m: e.tensor_tensor(out=uT[:, fc, :], in0=rl[m], in1=rl[m], op=ALU.mult),
                         reads=[b_rl[m]], writes=[b_uT[fc]])
            if tg + 1 < NTG3:
                norm_a(src_ap, s, tg + 1, TG, src_bufs, hbs3, b_hb)
            for j in range(NT):
                t0 = tg * TG + j * 128
                xi = cnt["tt"] % NXB
                cnt["tt"] += 1
                g.op("sp", lambda e, xi=xi, t0=t0: e.dma_start(out=xt[xi][:], in_=src_ap(s, t0)),
                     reads=src_bufs, writes=[b_xt[xi]], dma=True)
                for half in range(2):
                    bk, _, b_bk = next_bank()
                    for fc in range(32):
                        g.op("pe", lambda e, bk=bk, fc=fc, j=j, half=half: e.matmul(
                            bk[:], lhsT=uT[:, fc, j * 128:(j + 1) * 128], rhs=wdn[:, fc, half * 512:(half + 1) * 512],
                            start=(fc == 0), stop=(fc == 31)), reads=[b_uT[fc], b_wdn], writes=[b_bk])
                    hs = slice(half * 512, (half + 1) * 512)
                    g.op("dve", lambda e, bk=bk, hs=hs: e.tensor_tensor(out=yo1[:, hs], in0=bk[:], in1=GTm[:, hs], op=ALU.mult),
                         reads=[b_bk, b_GT], writes=[b_yo1])
                g.op("pool", lambda e, xi=xi: e.tensor_tensor(out=yo1[:], in0=yo1[:], in1=xt[xi][:], op=ALU.add),
                     reads=[b_yo1, b_xt[xi]], writes=[b_yo1])
                if not last:
                    g.op("sp", lambda e, t0=t0: e.dma_start(out=xres[s, t0:t0 + 128, :], in_=yo1[:]),
                         reads=[b_yo1], writes=[b_xres[s]], dma=True, multi=True)
                else:
                    m = j % 2
                    g.op("dve", lambda e, m=m: e.memset(fs[m][:], 0.0), writes=[b_fs[m]])
                    g.op("act", lambda e, m=m: e.activation(out=junk[:], in_=yo1[:], func=AF.Square, accum_out=fs[m][:, 0:1]),
                         reads=[b_yo1], writes=[b_junk, b_fs[m]])
                    g.op("dve", lambda e, m=m: e.tensor_scalar(out=fs[m][:, 1:2], in0=fs[m][:, 0:1], scalar1=1.0 / D, scalar2=EPS,
                                                               op0=ALU.mult, op1=ALU.add), reads=[b_fs[m]], writes=[b_fs[m]])
                    g.op("act", lambda e, m=m: e.activation(out=fs[m][:, 1:2], in_=fs[m][:, 1:2], func=AF.Sqrt), reads=[b_fs[m]], writes=[b_fs[m]])
                    g.op("dve", lambda e, m=m: e.reciprocal(out=fs[m][:, 1:2], in_=fs[m][:, 1:2]), reads=[b_fs[m]], writes=[b_fs[m]])
                    g.op("dve", lambda e, m=m: e.scalar_tensor_tensor(out=yo1[:], in0=yo1[:], scalar=fs[m][:, 1:2], in1=gf_bc[:],
                                                                      op0=ALU.mult, op1=ALU.mult), reads=[b_yo1, b_fs[m], b_gf], writes=[b_yo1])
                    finals.append(g.op("sp", lambda e, t0=t0: e.dma_start(out=y_out[s, t0:t0 + 128, :], in_=yo1[:]),
                                       reads=[b_yo1], dma=True))
            if tg + 1 < NTG3:
                norm_b(TG, hT[1 - k], b_hT[1 - k], hbs3, b_hb)

    if stage == "full":
        for l in range(DEPTH):
            for s in range(nseg):
                r1 = phase_inproj(l, s, src_is_input=(l == 0))
                r2 = phase_attn(l, s, dump=(debug and l == 0), outproj=True, dbg_l0=(debug and l == 0))
                if debug and l == 0:
                    finals.extend(r1)
                    finals.extend(r2)
                phase_mlp(l, s, src_is_input=False, last=(l == DEPTH - 1))
    elif stage == "attn":
        for s in range(nseg):
            finals.extend(phase_attn(0, s, dump=True))
    elif stage.startswith("inproj"):
        mode = {"inproj": "full", "inproj_nr": "none", "inproj_loads": "loads"}[stage]
        for s in range(nseg):
            finals.extend(phase_inproj(0, s, True, rope_mode=mode))
    else:
        for l in range(DEPTH):
            for s in range(nseg):
                phase_mlp(l, s, src_is_input=(l == 0), last=(l == DEPTH - 1))
    stats = g.emit(final_wait_ops=finals)
    stats["arena_KiB"] = {k: round(v / 1024, 1) for k, v in hiwater.items()}
    stats["arena_cap_KiB"] = ARENA / 1024
    return nc, stats


def _ident():
    import ml_dtypes
    return np.eye(128, dtype=np.float32).astype(ml_dtypes.bfloat16)


def _masks():
    import ml_dtypes
    kk = np.arange(128)[:, None]
    qq = np.arange(128)[None, :]
    m0, m1 = (kk >= qq), (kk <= qq)
    out = np.zeros((128, 4, 256), np.float32)
    for first in (0, 1):
        for last in (0, 1):
            a = m0 & (kk >= 64) if first else m0
            b = m1 & (kk < 64) if last else m1
            out[:, first + 2 * last, :128] = a
            out[:, first + 2 * last, 128:] = b
    return out.astype(ml_dtypes.bfloat16)


def _pool_consts(S):
    wins = (2, 4, 8, 16)
    iw = np.zeros((128, 2), np.float32)
    ic = np.zeros((128, 2, 2, 8), np.float32)
    for c in range(2):
        for half in range(2):
            w = wins[2 * c + half]
            rows = slice(half * 64, half * 64 + 64)
            iw[rows, c] = 1.0 / w
            for i in range(8):
                t = i
                ic[rows, c, 0, i] = 1.0 / (min(t + w // 2, S) - max(t - w // 2, 0))
                t = S - 8 + i
                ic[rows, c, 1, i] = 1.0 / (min(t + w // 2, S) - max(t - w // 2, 0))
    return iw, ic


def _ones64():
    import ml_dtypes
    return np.ones((128, 64), np.float32).astype(ml_dtypes.bfloat16)


def _rope(S):
    inv = (500000.0 ** (-np.arange(0, 16, 2, dtype=np.float32) / 16.0)).astype(np.float32)
    ang = np.arange(S, dtype=np.float32)[:, None] * inv[None, :]
    return np.ascontiguousarray(np.concatenate([np.cos(ang), np.sin(ang)], axis=1).astype(np.float32))


def host_consts(S):
    iw, ic = _pool_consts(S)
    return {"ident": _ident(), "rope": _rope(S), "masks": _masks(), "ones64": _ones64(), "pool_iw": iw, "pool_ic": ic}


def kernel(x_prompt, x_sample, c_prompt, c_sample, w_ada, b_ada, g_mix, g_mlp, w_in, pool_w, pool_scale,
           sgu_w, sgu_b, w_out, w_up, w_down, g_final):
    S = 8192
    nc, _ = build(S, 2)
    f = lambda a: np.ascontiguousarray(np.asarray(a, dtype=np.float32))
    x_prompt, x_sample, c_prompt, c_sample = f(x_prompt), f(x_sample), f(c_prompt), f(c_sample)
    shared = host_consts(S)
    shared.update({"w_ada": f(w_ada), "b_ada": f(b_ada), "g_mix": f(g_mix), "g_mlp": f(g_mlp), "w_in": f(w_in),
                   "w_up": f(w_up), "w_down": f(w_down), "g_final": f(g_final).reshape(1, D),
                   "pool_w": f(pool_w), "pool_scale": np.ascontiguousarray(f(pool_scale).reshape(DEPTH, 2, 128).transpose(0, 2, 1)),
                   "sgu_w": f(sgu_w), "sgu_b": f(sgu_b), "w_out": f(w_out)})
    in_maps = []
    for i in range(NCORES):
        sq, j = i // 4, i % 4
        x = np.stack([x_sample[i], x_prompt[sq, WIN_START[j]:WIN_START[j] + S]])
        c = np.stack([c_sample[i], c_prompt[sq]]).reshape(2, 8, 128).transpose(0, 2, 1)
        in_maps.append({"x": np.ascontiguousarray(x), "c": np.ascontiguousarray(c), **shared})
    res = run_bass_kernel_spmd(nc, in_maps, core_ids=list(range(NCORES)))
    y_sample = np.stack([res.results[i]["y"][0] for i in range(NCORES)])
    y_prompt = np.empty_like(x_prompt)
    for i in range(NCORES):
        sq, j = i // 4, i % 4
        y_prompt[sq, j * 4096:(j + 1) * 4096] = res.results[i]["y"][1][WIN_OWN[j]:WIN_OWN[j] + 4096]
    return (y_prompt, y_sample)
```
